# Optimizing a Trainium2 kernel written in Bass

```python
import math
import jax
import jax.numpy as jnp
from jax import lax
import numpy as np

D_MODEL = 2048
BATCH = 2
SEQ = 8192
DEPTH = 1
DEC_BATCH = 32
DEC_SEQ = 1
PAST_LEN = 16384
PAGE_SIZE = 128

ATT_HEAD_DIM = 128
ATT_HEADS_PER_GROUP = 4
ATT_GROUPS = ((128, 1), (512, 4), (2048, 16))
N_ATT_HEADS = ATT_HEADS_PER_GROUP * len(ATT_GROUPS)
ATT_WIDTH = N_ATT_HEADS * ATT_HEAD_DIM
ATT_OUT_WIDTH = ATT_HEADS_PER_GROUP * ATT_HEAD_DIM
M_HEADS = 4
M_QK_DIM = D_MODEL // 8
M_V_DIM = D_MODEL // 4
M_QK_WIDTH = M_HEADS * M_QK_DIM
M_V_WIDTH = M_HEADS * M_V_DIM
M_CONV = 4
M_CHUNK = 128
MEM_TOKENS = 256
MEM_HEADS = 4
MEM_HEAD_DIM = 128
MEM_WIDTH = MEM_HEADS * MEM_HEAD_DIM
D_FF = 4 * D_MODEL
ALPHA = (2 * DEPTH) ** 0.25
BETA = (8 * DEPTH) ** -0.25
LN_EPS = 1e-5
COL_SIZES = (ATT_WIDTH, ATT_WIDTH, ATT_WIDTH, 2 * M_QK_WIDTH, M_V_WIDTH, M_V_WIDTH, M_HEADS, M_HEADS, D_MODEL, D_MODEL)
COL_SPLITS = tuple(sum(COL_SIZES[:i + 1]) for i in range(len(COL_SIZES) - 1))
N_IN = sum(COL_SIZES)

kernel_name = 'hybrid_dilated_attn_mlstm_decoder_step'


def _layernorm(x, g, b):
    xf = x.astype(jnp.float32)
    mu = xf.mean(-1, keepdims=True)
    var = jnp.square(xf - mu).mean(-1, keepdims=True)
    return ((xf - mu) * lax.rsqrt(var + LN_EPS) * g + b).astype(x.dtype)


def _dilated_prompt(q, k, v, dilation, span):
    B, T, H, E = q.shape
    blk = span
    unit = dilation * blk
    Tp = -(-T // unit) * unit
    nb = Tp // unit

    def to_blocks(a):
        a = jnp.pad(a.astype(jnp.float32), ((0, 0), (0, Tp - T), (0, 0), (0, 0)))
        a = a.reshape(B, Tp // dilation, dilation, H, E).transpose(0, 2, 1, 3, 4)
        return a.reshape(B, dilation, nb, blk, H, E)

    def with_prev(a):
        prev = jnp.pad(a, ((0, 0), (0, 0), (1, 0), (0, 0), (0, 0), (0, 0)))[:, :, :-1]
        return jnp.concatenate([prev, a], axis=3)

    qb, kb, vb = to_blocks(q), to_blocks(k), to_blocks(v)
    kk, vv = with_prev(kb), with_prev(vb)
    s = jnp.einsum('brnqhe,brnkhe->brnhqk', qb, kk) * E ** -0.5
    qi = jnp.arange(blk)[:, None]
    ki = jnp.arange(2 * blk)[None, :]
    band = (ki >= qi) & (ki <= qi + blk)
    not_first = jnp.arange(nb)[:, None, None] > 0
    mask = band[None] & (not_first | (ki >= blk)[None])
    s = jnp.where(mask[None, None, :, None], s, -jnp.inf)
    lse = jax.nn.logsumexp(s, axis=-1)
    p = jnp.exp(s - lse[..., None])
    o = jnp.einsum('brnhqk,brnkhe->brnqhe', p, vv)
    o = o.reshape(B, dilation, nb * blk, H, E).transpose(0, 2, 1, 3, 4).reshape(B, Tp, H, E)[:, :T]
    lse = lse.transpose(0, 1, 2, 4, 3).reshape(B, dilation, nb * blk, H).transpose(0, 2, 1, 3).reshape(B, Tp, H)[:, :T]
    return o, lse


def _dilated_step(q, k_cat, v_cat, dilation, span):
    L = q.shape[1]
    E = q.shape[-1]
    Wb = k_cat.shape[1] - L
    idx = (Wb + jnp.arange(L))[:, None] - dilation * jnp.arange(span + 1)[None, :]
    valid = idx >= 0
    idx = jnp.maximum(idx, 0)
    kg = k_cat[:, idx].astype(jnp.float32)
    vg = v_cat[:, idx].astype(jnp.float32)
    s = jnp.einsum('blhe,bljhe->bhlj', q.astype(jnp.float32), kg) * E ** -0.5
    s = jnp.where(valid[None, None], s, -jnp.inf)
    lse = jax.nn.logsumexp(s, axis=-1)
    p = jnp.exp(s - lse[..., None])
    o = jnp.einsum('bhlj,bljhe->blhe', p, vg)
    return o, lse.transpose(0, 2, 1)


def _mlstm_chunk(carry, blk):
    c0, n0, m0 = carry
    q, k, v, ig, lf = blk
    L = q.shape[2]
    b = jnp.cumsum(lf, axis=-1)
    causal = jnp.tril(jnp.ones((L, L), dtype=bool))
    dmat = jnp.where(causal, b[..., :, None] - b[..., None, :] + ig[..., None, :], -jnp.inf)
    inter = b + m0[..., None]
    m_t = jnp.maximum(inter, dmat.max(-1))
    s = jnp.einsum('bhtd,bhsd->bhts', q, k) * jnp.exp(dmat - m_t[..., None])
    w_inter = jnp.exp(inter - m_t)
    num = jnp.einsum('bhts,bhsv->bhtv', s, v) + w_inter[..., None] * jnp.einsum('bhtd,bhdv->bhtv', q, c0)
    den = s.sum(-1) + w_inter * jnp.einsum('bhtd,bhd->bht', q, n0)
    h = num / jnp.maximum(jnp.abs(den), jnp.exp(-m_t))[..., None]
    g = b[..., -1:] - b + ig
    m_new = jnp.maximum(b[..., -1] + m0, g.max(-1))
    wk = jnp.exp(g - m_new[..., None])
    decay = jnp.exp(b[..., -1] + m0 - m_new)
    c_new = decay[..., None, None] * c0 + jnp.einsum('bhs,bhsd,bhsv->bhdv', wk, k, v)
    n_new = decay[..., None] * n0 + jnp.einsum('bhs,bhsd->bhd', wk, k)
    return (c_new, n_new, m_new), h


def _mlstm_seq(q, k, v, ig, lf, state):
    B, H, T, _ = q.shape
    L = math.gcd(T, M_CHUNK)
    n = T // L

    def chunks(a):
        return jnp.moveaxis(a.reshape((B, H, n, L) + a.shape[3:]), 2, 0)

    xs = (chunks(q), chunks(k), chunks(v), chunks(ig), chunks(lf))
    state, h = lax.scan(_mlstm_chunk, state, xs)
    h = jnp.moveaxis(h, 0, 2).reshape(B, H, T, -1)
    return h, state


def _layer(x, prm, mem=None, kv_bufs=None, conv_buf=None, m_state=None, mem_kv=None):
    (w_in, b_ig, b_fg, w_conv, b_conv, g_mnorm, w_br_a, w_br_m, w_mix_o, ln1_g, ln1_b,
     w_cq, w_mk, w_mv, w_co, ln2_g, ln2_b, w_up, w_down, ln3_g, ln3_b) = prm
    f32 = jnp.float32
    B, T, _ = x.shape
    aq, ak, av, mqk, mv, mo, mi, mf, ga, gm = jnp.split(x @ w_in, COL_SPLITS, axis=-1)

    aq = aq.reshape(B, T, N_ATT_HEADS, ATT_HEAD_DIM)
    ak = ak.reshape(B, T, N_ATT_HEADS, ATT_HEAD_DIM)
    av = av.reshape(B, T, N_ATT_HEADS, ATT_HEAD_DIM)
    outs, lses, new_kv = [], [], []
    for g, (win, dil) in enumerate(ATT_GROUPS):
        hs = slice(g * ATT_HEADS_PER_GROUP, (g + 1) * ATT_HEADS_PER_GROUP)
        q_g, k_g, v_g = aq[:, :, hs], ak[:, :, hs], av[:, :, hs]
        if kv_bufs is None:
            o_g, l_g = _dilated_prompt(q_g, k_g, v_g, dil, win // dil)
            keep = min(win, T)
            new_kv.append(jnp.stack([k_g[:, T - keep:], v_g[:, T - keep:]], axis=2))
        else:
            buf = kv_bufs[g].astype(k_g.dtype)
            k_cat = jnp.concatenate([buf[:, :, 0], k_g], axis=1)
            v_cat = jnp.concatenate([buf[:, :, 1], v_g], axis=1)
            o_g, l_g = _dilated_step(q_g, k_cat, v_cat, dil, win // dil)
            new_kv.append(jnp.stack([k_g, v_g], axis=2))
        outs.append(o_g)
        lses.append(l_g)
    wg = jax.nn.softmax(jnp.stack(lses), axis=0)
    att = jnp.einsum('gbth,gbthe->bthe', wg, jnp.stack(outs)).reshape(B, T, ATT_OUT_WIDTH)

    if conv_buf is None:
        conv_buf = jnp.zeros((B, M_CONV - 1, 2 * M_QK_WIDTH), mqk.dtype)
    ext = jnp.concatenate([conv_buf.astype(mqk.dtype), mqk], axis=1)
    conv = sum((w_conv[j] * ext[:, j:j + T] for j in range(M_CONV)), b_conv)
    new_conv = ext[:, T:]
    mq, mk = jnp.split(jax.nn.silu(conv), 2, axis=-1)

    def heads(a, e):
        return a.reshape(B, T, M_HEADS, e).transpose(0, 2, 1, 3).astype(f32)

    q_m = heads(mq, M_QK_DIM)
    k_m = heads(mk, M_QK_DIM) * M_QK_DIM ** -0.5
    v_m = heads(mv, M_V_DIM)
    ig = (mi + b_ig).astype(f32).transpose(0, 2, 1)
    lf = jax.nn.log_sigmoid((mf + b_fg).astype(f32)).transpose(0, 2, 1)
    if m_state is None:
        m_state = (jnp.zeros((B, M_HEADS, M_QK_DIM, M_V_DIM), f32),
                   jnp.zeros((B, M_HEADS, M_QK_DIM), f32),
                   jnp.zeros((B, M_HEADS), f32))
    else:
        m_state = (m_state[0].astype(f32), m_state[1].astype(f32), m_state[2].astype(f32))
    h, (c_new, n_new, m_new) = _mlstm_seq(q_m, k_m, v_m, ig, lf, m_state)
    mu = h.mean(-1, keepdims=True)
    var = jnp.square(h - mu).mean(-1, keepdims=True)
    h = ((h - mu) * lax.rsqrt(var + LN_EPS)).transpose(0, 2, 1, 3).reshape(B, T, M_V_WIDTH)
    h_m = (h * g_mnorm * jax.nn.sigmoid(mo.astype(f32))).astype(x.dtype)

    merged = (jax.nn.sigmoid(ga) * (att.astype(x.dtype) @ w_br_a)
              + jax.nn.sigmoid(gm) * (h_m @ w_br_m))
    x = _layernorm(ALPHA * x + merged @ w_mix_o, ln1_g, ln1_b)

    if mem_kv is None:
        M = mem.shape[1]
        mem_kv = jnp.stack([(mem @ w_mk).reshape(B, M, MEM_HEADS, MEM_HEAD_DIM),
                            (mem @ w_mv).reshape(B, M, MEM_HEADS, MEM_HEAD_DIM)], axis=2)
    qc = (x @ w_cq).reshape(B, T, MEM_HEADS, MEM_HEAD_DIM)
    sc = jnp.einsum('bthe,bmhe->bhtm', qc, mem_kv[:, :, 0].astype(qc.dtype)).astype(f32) * MEM_HEAD_DIM ** -0.5
    pc = jax.nn.softmax(sc, axis=-1)
    oc = jnp.einsum('bhtm,bmhe->bthe', pc, mem_kv[:, :, 1].astype(f32)).reshape(B, T, MEM_WIDTH).astype(x.dtype)
    x = _layernorm(ALPHA * x + oc @ w_co, ln2_g, ln2_b)

    hid = jnp.square(jax.nn.relu(x @ w_up))
    x = _layernorm(ALPHA * x + hid @ w_down, ln3_g, ln3_b)
    return x, new_kv, new_conv, (c_new, n_new, m_new), mem_kv


def setup_inputs(seed: int = 0) -> dict:
    key = jax.random.key(seed)
    ks = iter(jax.random.split(key, 40))
    L = DEPTH

    def nrm(shape, scale=1.0):
        return jax.random.normal(next(ks), shape, jnp.float32) * scale

    def gain(n):
        return 1.0 + nrm((L, n), 0.02)

    def kv_shape(win):
        return (L, DEC_BATCH, min(win, PAST_LEN), 2, ATT_HEADS_PER_GROUP, ATT_HEAD_DIM)

    return {
        'x_prompt': nrm((BATCH, SEQ, D_MODEL)),
        'x_sample': nrm((DEC_BATCH, DEC_SEQ, D_MODEL)),
        'mem_prompt': nrm((BATCH, MEM_TOKENS, D_MODEL)),
        'cache_kv_g0': nrm(kv_shape(ATT_GROUPS[0][0])),
        'cache_kv_g1': nrm(kv_shape(ATT_GROUPS[1][0])),
        'cache_kv_g2': nrm(kv_shape(ATT_GROUPS[2][0])),
        'cache_mem_kv': nrm((L, DEC_BATCH, MEM_TOKENS, 2, MEM_HEADS, MEM_HEAD_DIM)),
        'state_mlstm_conv': nrm((L, DEC_BATCH, M_CONV - 1, 2 * M_QK_WIDTH)),
        'state_mlstm_c': nrm((L, DEC_BATCH, M_HEADS, M_QK_DIM, M_V_DIM), 0.5),
        'state_mlstm_n': nrm((L, DEC_BATCH, M_HEADS, M_QK_DIM), 0.5),
        'state_mlstm_m': nrm((L, DEC_BATCH, M_HEADS)),
        'w_in': nrm((L, D_MODEL, N_IN), D_MODEL ** -0.5),
        'b_igate': nrm((L, M_HEADS), 0.1),
        'b_fgate': jnp.linspace(3.0, 6.0, M_HEADS)[None] + nrm((L, M_HEADS), 0.1),
        'w_conv': nrm((L, M_CONV, 2 * M_QK_WIDTH), M_CONV ** -0.5),
        'b_conv': nrm((L, 2 * M_QK_WIDTH), 0.02),
        'g_mlstm_norm': gain(M_V_WIDTH),
        'w_branch_att': nrm((L, ATT_OUT_WIDTH, D_MODEL), ATT_OUT_WIDTH ** -0.5),
        'w_branch_mlstm': nrm((L, M_V_WIDTH, D_MODEL), M_V_WIDTH ** -0.5),
        'w_mix_out': nrm((L, D_MODEL, D_MODEL), BETA * D_MODEL ** -0.5),
        'ln1_g': gain(D_MODEL),
        'ln1_b': nrm((L, D_MODEL), 0.02),
        'w_cross_q': nrm((L, D_MODEL, MEM_WIDTH), D_MODEL ** -0.5),
        'w_mem_k': nrm((L, D_MODEL, MEM_WIDTH), D_MODEL ** -0.5),
        'w_mem_v': nrm((L, D_MODEL, MEM_WIDTH), D_MODEL ** -0.5),
        'w_cross_out': nrm((L, MEM_WIDTH, D_MODEL), BETA * MEM_WIDTH ** -0.5),
        'ln2_g': gain(D_MODEL),
        'ln2_b': nrm((L, D_MODEL), 0.02),
        'w_up': nrm((L, D_MODEL, D_FF), D_MODEL ** -0.5),
        'w_down': nrm((L, D_FF, D_MODEL), BETA * D_FF ** -0.5),
        'ln3_g': gain(D_MODEL),
        'ln3_b': nrm((L, D_MODEL), 0.02),
    }


def reference(x_prompt, x_sample, mem_prompt, cache_kv_g0, cache_kv_g1, cache_kv_g2, cache_mem_kv,
              state_mlstm_conv, state_mlstm_c, state_mlstm_n, state_mlstm_m,
              w_in, b_igate, b_fgate, w_conv, b_conv, g_mlstm_norm, w_branch_att, w_branch_mlstm,
              w_mix_out, ln1_g, ln1_b, w_cross_q, w_mem_k, w_mem_v, w_cross_out, ln2_g, ln2_b,
              w_up, w_down, ln3_g, ln3_b):
    weights = (w_in, b_igate, b_fgate, w_conv, b_conv, g_mlstm_norm, w_branch_att, w_branch_mlstm,
               w_mix_out, ln1_g, ln1_b, w_cross_q, w_mem_k, w_mem_v, w_cross_out, ln2_g, ln2_b,
               w_up, w_down, ln3_g, ln3_b)
    xp, xs = x_prompt, x_sample
    p_kv0, p_kv1, p_kv2, p_mem, p_conv, p_c, p_n, p_m = [], [], [], [], [], [], [], []
    s_kv0, s_kv1, s_kv2, s_conv, s_c, s_n, s_m = [], [], [], [], [], [], []
    for layer in range(DEPTH):
        prm = tuple(w[layer] for w in weights)
        xp, kv, conv, (c, n, m), mkv = _layer(xp, prm, mem=mem_prompt)
        p_kv0.append(kv[0])
        p_kv1.append(kv[1])
        p_kv2.append(kv[2])
        p_mem.append(mkv)
        p_conv.append(conv)
        p_c.append(c)
        p_n.append(n)
        p_m.append(m)
        xs, kv, conv, (c, n, m), _ = _layer(
            xs, prm,
            kv_bufs=(cache_kv_g0[layer], cache_kv_g1[layer], cache_kv_g2[layer]),
            conv_buf=state_mlstm_conv[layer],
            m_state=(state_mlstm_c[layer], state_mlstm_n[layer], state_mlstm_m[layer]),
            mem_kv=cache_mem_kv[layer])
        s_kv0.append(kv[0])
        s_kv1.append(kv[1])
        s_kv2.append(kv[2])
        s_conv.append(conv)
        s_c.append(c)
        s_n.append(n)
        s_m.append(m)
    st = jnp.stack
    return (xp, xs,
            st(p_kv0), st(p_kv1), st(p_kv2), st(p_mem), st(p_conv), st(p_c), st(p_n), st(p_m),
            st(s_kv0), st(s_kv1), st(s_kv2), st(s_conv), st(s_c), st(s_n), st(s_m))
```

```python
import numpy as np
from contextlib import ExitStack
import concourse.bass as bass
import concourse.mybir as mybir

F32 = mybir.dt.float32
BF16 = mybir.dt.bfloat16
I32 = mybir.dt.int32
AF = mybir.ActivationFunctionType
ALU = mybir.AluOpType
AX = mybir.AxisListType

ENGS = ("pe", "act", "dve", "pool", "sp")


class Buf:
    __slots__ = ("name", "w", "r", "sem", "dcount", "excl")

    registry = None
    birth = None

    def __init__(self, name="", excl=False):
        self.name = name
        self.excl = excl
        self.w = Buf.birth
        if Buf.registry is not None:
            Buf.registry.append(self)
        self.r = []
        self.sem = None
        self.dcount = 0


class T:
    __slots__ = ("bufs", "ap")

    def __init__(self, buf, ap):
        self.bufs = list(buf) if isinstance(buf, (list, tuple)) else [buf]
        self.ap = ap

    @property
    def buf(self):
        return self.bufs[0]

    def __getitem__(self, idx):
        return T(self.bufs, self.ap[idx])

    def v(self, ap):
        return T(self.bufs, ap)


def _bufs(items):
    out = []
    for t in items:
        if isinstance(t, T):
            out.extend(t.bufs)
        else:
            out.append(t)
    return out


class KB:
    def __init__(self, nc, n_dma_sems=120):
        self.nc = nc
        self.es = ExitStack()
        self.ops = {e: [] for e in ENGS}
        self.flag = {e: [] for e in ENGS}
        self.waited = {e: {} for e in ENGS}
        self.n_dma_sems = n_dma_sems
        self.dma_sem_used = 0
        self.esem = {}
        self.dsem = []
        self.uid = 0
        self.scopes = []
        self.n_cc = 0
        Buf.registry = None
        Buf.birth = None
        self._scr = self.tile([128, 8], F32, "kb_scratch")

    def push_scope(self):
        self.scopes.append((self.es, Buf.registry))
        self.es = ExitStack()
        Buf.registry = []

    def pop_scope(self):
        bufs = list(Buf.registry)
        scr = self._scr
        self.op("dve", lambda h: h.memset(scr.ap[:, 0:1], 0.0), reads=bufs, writes=bufs + [scr])
        tok = ("e", "dve", len(self.ops["dve"]) - 1)
        self.es.close()
        self.es, Buf.registry = self.scopes.pop()
        Buf.birth = tok

    def collective(self, kind, in_h, out_h, groups, reads=(), writes=()):
        assert self.dma_sem_used < self.n_dma_sems
        sem = self.dma_sem_used
        self.dma_sem_used += 1
        waits = self._deps("pool", _bufs(reads), _bufs(writes))
        tok = ("d", sem, 1)
        self.ops["pool"].append(("cc", (kind, in_h, out_h, groups), waits, sem, {}))
        self.flag["pool"].append(False)
        self._commit(tok, _bufs(reads), _bufs(writes))

    def sb(self, shape, dtype, name=None):
        self.uid += 1
        return self.es.enter_context(self.nc.sbuf_tensor(name or f"sb{self.uid}", list(shape), dtype))

    def ps(self, shape, dtype, name=None):
        self.uid += 1
        return self.es.enter_context(self.nc.psum_tensor(name or f"ps{self.uid}", list(shape), dtype))

    def tile(self, shape, dtype, name=None):
        t = self.sb(shape, dtype, name)
        return T(Buf(name or ""), t[tuple(slice(None) for _ in shape)])

    def ptile(self, shape, dtype, name=None):
        t = self.ps(shape, dtype, name)
        return T(Buf(name or "", excl=True), t[tuple(slice(None) for _ in shape)])

    def _deps(self, eng, reads, writes):
        toks = []
        for b in _bufs(reads):
            if b.w is not None:
                toks.append(b.w)
        for b in _bufs(writes):
            if b.w is not None:
                toks.append(b.w)
            toks.extend(b.r)
        waits = []
        wd = self.waited[eng]
        for tok in toks:
            kind, src, val = tok
            if kind == "e" and src == eng and eng == "pe":
                continue
            key = (kind, src)
            if wd.get(key, -1) >= val:
                continue
            wd[key] = val
            if kind == "e":
                self.flag[src][val] = True
            waits.append(tok)
        best = {}
        for tok in waits:
            key = (tok[0], tok[1])
            if key not in best or best[key][2] < tok[2]:
                best[key] = tok
        return list(best.values())

    def _commit(self, tok, reads, writes):
        for b in _bufs(reads):
            b.r.append(tok)
        for b in _bufs(writes):
            b.w = tok
            b.r = []

    def op(self, eng, fn, reads=(), writes=()):
        rb, wb = _bufs(reads), _bufs(writes)
        writes = wb + [b for b in rb if b.excl and b not in wb]
        reads = [b for b in rb if not b.excl]
        waits = self._deps(eng, reads, writes)
        idx = len(self.ops[eng])
        self.ops[eng].append(("c", fn, waits))
        self.flag[eng].append(False)
        self._commit(("e", eng, idx), reads, writes)

    def dma(self, queue, out_ap, in_ap, reads=(), writes=(), **kw):
        pairs = out_ap if isinstance(out_ap, list) else [(out_ap, in_ap)]
        tb = _bufs(list(writes) + list(reads))[0]
        if tb.sem is None:
            tb.sem = self.dma_sem_used
            self.dma_sem_used += 1
            assert self.dma_sem_used <= self.n_dma_sems, "out of dma sems"
        waits = self._deps(queue, reads, writes)
        tb.dcount += 16 * len(pairs)
        tok = ("d", tb.sem, tb.dcount)
        idx = len(self.ops[queue])
        self.ops[queue].append(("d", pairs, waits, tb.sem, kw))
        self.flag[queue].append(False)
        self._commit(tok, reads, writes)
        return tok

    def wait_all(self, eng, bufs):
        waits = self._deps(eng, [], _bufs(bufs))
        self.ops[eng].append(("c", None, waits))
        self.flag[eng].append(False)

    def emit(self):
        nc = self.nc
        with ExitStack() as es:
            for e in ENGS:
                self.esem[e] = es.enter_context(nc.semaphore(f"prog_{e}"))
            for i in range(self.dma_sem_used):
                self.dsem.append(es.enter_context(nc.semaphore(f"dma_{i}")))
            evno = {}
            for e in ENGS:
                c = 0
                arr = []
                for f in self.flag[e]:
                    if f:
                        c += 1
                    arr.append(c)
                evno[e] = arr
            block = es.enter_context(nc.Block())

            def replay(e, h):
                for i, rec in enumerate(self.ops[e]):
                    waits = rec[2]
                    for kind, src, val in waits:
                        if kind == "e":
                            assert self.flag[src][val]
                            h.wait_ge(self.esem[src], evno[src][val])
                        else:
                            h.wait_ge(self.dsem[src], val)
                    if rec[0] == "c":
                        if rec[1] is None:
                            continue
                        inst = rec[1](h)
                        if self.flag[e][i]:
                            inst.then_inc(self.esem[e], 1)
                    elif rec[0] == "cc":
                        kind, in_h, out_h, groups = rec[1]
                        h.collective_compute(kind, mybir.AluOpType.bypass, replica_groups=groups,
                                             ins=[in_h.ap().opt()], outs=[out_h.ap().opt()]).then_inc(self.dsem[rec[3]])
                    else:
                        for (o, a) in rec[1]:
                            h.dma_start(out=o, in_=a, **rec[4]).then_inc(self.dsem[rec[3]], 16)

            @block.tensor
            def _(h):
                replay("pe", h)

            @block.scalar
            def _(h):
                replay("act", h)

            @block.vector
            def _(h):
                replay("dve", h)

            @block.gpsimd
            def _(h):
                replay("pool", h)

            @block.sync
            def _(h):
                replay("sp", h)
        self.es.close()

    def mm(self, out, lhsT, rhs, start=True, stop=True):
        self.op("pe", lambda h: h.matmul(out.ap, lhsT.ap, rhs.ap, start=start, stop=stop),
                reads=[lhsT, rhs] + ([] if start else [out]), writes=[out])

    def transpose(self, out, in_, ident):
        self.op("pe", lambda h: h.transpose(out.ap, in_.ap, ident.ap), reads=[in_, ident], writes=[out])

    def act(self, out, in_, func, bias=None, scale=1.0, eng="act", extra_reads=()):
        rd = [in_] + list(extra_reads)
        kw = {}
        if bias is not None:
            if isinstance(bias, T):
                rd.append(bias)
                kw["bias"] = bias.ap
            else:
                kw["bias"] = bias
        if isinstance(scale, T):
            rd.append(scale)
            kw["scale"] = scale.ap
        else:
            kw["scale"] = scale
        self.op("act", lambda h: h.activation(out.ap, in_.ap, func, **kw), reads=rd, writes=[out])

    def tt(self, eng, out, in0, in1, op):
        self.op(eng, lambda h: h.tensor_tensor(out.ap, in0.ap, in1.ap, op), reads=[in0, in1], writes=[out])

    def ts(self, eng, out, in0, s1, s2, op0, op1=None):
        rd = [in0]
        a1 = s1.ap if isinstance(s1, T) else s1
        a2 = s2.ap if isinstance(s2, T) else s2
        if isinstance(s1, T):
            rd.append(s1)
        if isinstance(s2, T):
            rd.append(s2)
        if op1 is None:
            self.op(eng, lambda h: h.tensor_scalar(out.ap, in0.ap, a1, None, op0), reads=rd, writes=[out])
        else:
            self.op(eng, lambda h: h.tensor_scalar(out.ap, in0.ap, a1, a2, op0, op1), reads=rd, writes=[out])

    def stt(self, eng, out, in0, scalar, in1, op0, op1):
        rd = [in0, in1]
        a = scalar.ap if isinstance(scalar, T) else scalar
        if isinstance(scalar, T):
            rd.append(scalar)
        self.op(eng, lambda h: h.scalar_tensor_tensor(out.ap, in0.ap, a, in1.ap, op0, op1), reads=rd, writes=[out])

    def copy(self, eng, out, in_):
        if eng == "act":
            self.op(eng, lambda h: h.copy(out.ap, in_.ap), reads=[in_], writes=[out])
        else:
            self.op(eng, lambda h: h.tensor_copy(out.ap, in_.ap), reads=[in_], writes=[out])

    def memset(self, eng, out, val):
        self.op(eng, lambda h: h.memset(out.ap, val), reads=[], writes=[out])

    def reduce(self, eng, out, in_, op, axis=AX.X):
        self.op(eng, lambda h: h.tensor_reduce(out.ap, in_.ap, axis, op), reads=[in_], writes=[out])


D = 2048
ALPHA = 2.0 ** 0.25
LN_EPS = 1e-5
KC = 16


def l2_io(nc, NP, NS, DFF, io=None, fused=False):
    io = {} if io is None else io
    NT = NP + NS

    def din(name, shape, dt=F32):
        io[name] = nc.dram_tensor(name, list(shape), dt, kind="ExternalInput").ap()

    def dout(name, shape, dt=F32):
        io[name] = nc.dram_tensor(name, list(shape), dt, kind="ExternalOutput").ap()

    din("x2T", [D, NT])
    if fused:
        din("selq", [128, 4, 128], BF16)
        din("sels", [128, 8, 128], BF16)
    else:
        din("attT", [512, NT], BF16)
        din("hmT", [D, NT], BF16)
    din("wg", [D, 2 * D])
    din("w_br_a", [512, D])
    din("w_br_m", [D, D])
    din("w_mix", [D, D])
    din("w_cq", [D, 512])
    din("w_mk", [D, 512])
    din("w_mv", [D, 512])
    din("w_co", [512, D])
    din("w_up", [D, DFF])
    din("w_down", [DFF, D])
    din("memT", [D, 256])
    din("skv", [max(NS, 1), 256, 2, 512])
    din("lnp", [128, 6, KC])
    din("ident", [128, 128])
    dout("yT", [D, NT])
    dout("pmem", [256, 2, 512])
    return io


def build_l2(NP=2048, NS=4, DFF=8192, ST=1024):
    nc = bass.Bass("TRN2", target_bir_lowering=False)
    io = l2_io(nc, NP, NS, DFF)
    k = KB(nc)
    sh = {"pss": [k.ptile([128, 512], F32, f"ps{i}") for i in range(6)], "pi": [0]}
    emit_l2(k, io, sh, NP, NS, DFF, ST)
    k.emit()
    return nc


def emit_l2(k, io, sh, NP=2048, NS=4, DFF=8192, ST=1024, G=None):
    NT = NP + NS
    FC = DFF // 128
    xT = io["x2T"]
    if G is None:
        attT, hmT = io["attT"], io["hmT"]
    wg, w_br_a, w_br_m, w_mix, w_cq, w_mk, w_mv, w_co, w_up, w_down = (io[n_] for n_ in (
        "wg", "w_br_a", "w_br_m", "w_mix", "w_cq", "w_mk", "w_mv", "w_co", "w_up", "w_down"))
    memT, skv, lnp, ident_d, yT, pmem = (io[n_] for n_ in ("memT", "skv", "lnp", "ident", "yT", "pmem"))

    TTM = ST + NS
    assert TTM % 2 == 0
    XH = k.sb([128, 2 * KC * TTM], BF16, "XH")
    mg_t = k.sb([128, KC, TTM], BF16, "mg")
    G_t = k.sb([128, 8, TTM], BF16, "G")
    xin = [T(Buf(f"xin{i}"), XH[:, i * TTM:(i + 1) * TTM]) for i in range(KC)]
    hmi = [T(Buf(f"hmi{i}"), XH[:, (KC + i) * TTM:(KC + i + 1) * TTM]) for i in range(KC)]
    _al = xin + hmi
    resid = [T([Buf(f"resid{i}"), _al[2 * i].buf, _al[2 * i + 1].buf],
               XH[:, 2 * i * TTM:(2 * i + 2) * TTM].bitcast(F32)) for i in range(KC)]
    mg = [T(Buf(f"mg{i}"), mg_t[:, i, :]) for i in range(KC)]
    xbf = mg
    hg = [T(Buf(f"G{i}"), G_t[:, i, :]) for i in range(8)]
    atc = hg[0:4]
    occ = hg[4:8]
    NW = 5
    wslots = [k.tile([128, KC, 128], BF16, f"w{i}") for i in range(NW)]
    wi = [0]
    pss, pi = sh["pss"], sh["pi"]
    xst = [k.tile([128, 512], F32, f"xst{i}") for i in range(4)]
    xsi = [0]
    tmpf = [k.tile([128, 512], F32, f"tmpf{i}") for i in range(4)]
    tfi = [0]
    tmpb = [k.tile([128, 512], BF16, f"tmpb{i}") for i in range(4)]
    tbi = [0]
    consts = Buf("consts")
    lnp_t = T(consts, k.sb([128, 6, KC], F32, "lnp_sb")[:, :, :])
    ident_f = T(consts, k.sb([128, 128], F32, "ident_sb")[:, :])
    ident_b = k.tile([128, 128], BF16, "identb2")
    ones_b = k.tile([128, 128], BF16, "onesb2")
    mean_bc = k.tile([128, 512], F32, "mean")
    rstd_bc = k.tile([128, 512], F32, "rstd")
    memT_b = k.tile([128, KC, 256], BF16, "memTb")
    kT_b = k.tile([128, 4, 256], BF16, "kTb")
    v_b = k.tile([128, 2, 512], BF16, "vb")
    skv_f = k.tile([128, 2, 512], F32, "skvf")
    skT_b = k.tile([128, 4, 256], BF16, "skTb")
    sv_b = k.tile([128, 2, 512], BF16, "svb")
    yst = xst
    ysi = xsi

    def nxt(lst, ctr):
        t = lst[ctr[0] % len(lst)]
        ctr[0] += 1
        return t

    if G is not None:
        selq_t = k.tile([128, 4, 128], BF16, "selq_t")
        k.dma("sp", selq_t.ap, io["selq"], writes=[selq_t])
        sels_t = k.tile([128, 8, 128], BF16, "sels_t")
        k.dma("sp", sels_t.ap, io["sels"], writes=[sels_t])
        cands = [k.tile([128, 4, 512], BF16, f"cand{i}") for i in range(2)]
        cdi = [0]
        cands8 = [k.tile([128, 8, 4], BF16, f"cand8_{i}") for i in range(2)]
        cd8i = [0]
    k.dma("sp", lnp_t.ap, lnp, writes=[lnp_t])
    k.dma("sp", ident_f.ap, ident_d, writes=[ident_f])
    k.copy("dve", ident_b, ident_f)
    k.memset("dve", ones_b, 1.0)

    def wview(W, r0, nk, c0, ncol=128):
        return W[r0:r0 + nk * 128, c0:c0 + ncol].rearrange("(kc p) n -> p kc n", p=128)

    def load_w(W, r0, nk, c0):
        ws = nxt(wslots, wi)
        k.dma("pool", ws.ap[:, 0:nk, :], wview(W, r0, nk, c0), writes=[ws])
        return ws

    def group(ws, nk, ins, off, ln):
        ps = nxt(pss, pi)
        for kc in range(nk):
            k.mm(ps[:, 0:ln], ws[:, kc, :], ins[kc][:, off:off + ln], start=(kc == 0), stop=(kc == nk - 1))
        return ps

    k.dma("pool", memT_b.ap, memT.rearrange("(kc p) n -> p kc n", p=128), writes=[memT_b])
    for kv, W in enumerate((w_mk, w_mv)):
        for hh in range(4):
            ws = load_w(W, 0, KC, hh * 128)
            if kv == 0:
                ps = nxt(pss, pi)
                for kc in range(KC):
                    k.mm(ps[:, 0:256], ws[:, kc, :], memT_b[:, kc, :], start=(kc == 0), stop=(kc == KC - 1))
                k.copy("dve", kT_b[:, hh, :], ps[:, 0:256])
            for mc in range(2):
                ps = nxt(pss, pi)
                for kc in range(KC):
                    k.mm(ps[:, 0:128], memT_b[:, kc, mc * 128:(mc + 1) * 128], ws[:, kc, :], start=(kc == 0), stop=(kc == KC - 1))
                ko = nxt(xst, xsi)
                k.copy("dve", ko[:, 0:128], ps[:, 0:128])
                if kv == 1:
                    k.copy("act", v_b[:, mc, hh * 128:(hh + 1) * 128], ps[:, 0:128])
                k.dma("sp", pmem[mc * 128:(mc + 1) * 128, kv, hh * 128:(hh + 1) * 128], ko.ap[:, 0:128], reads=[ko])

    def cross_attn(qc, kT, vv, outc, off, ln):
        scale = 128.0 ** -0.5
        for hh in range(4):
            pT = []
            for mc in range(2):
                ps = nxt(pss, pi)
                k.mm(ps[:, 0:ln], kT[:, hh, mc * 128:(mc + 1) * 128], qc[hh][:, off:off + ln])
                pb = nxt(tmpb, tbi)
                k.act(pb[:, 0:ln], ps[:, 0:ln], AF.Exp, scale=scale)
                pT.append(pb)
            psd = nxt(pss, pi)
            for mc in range(2):
                k.mm(psd[:, 0:ln], ones_b, pT[mc][:, 0:ln], start=(mc == 0), stop=(mc == 1))
            rden = nxt(tmpf, tfi)
            k.op("dve", lambda h, o=rden[:, 0:ln], i=psd[:, 0:ln]: h.reciprocal(o.ap, i.ap), reads=[psd], writes=[rden])
            pso = nxt(pss, pi)
            for mc in range(2):
                k.mm(pso[:, 0:ln], vv[:, mc, hh * 128:(hh + 1) * 128], pT[mc][:, 0:ln], start=(mc == 0), stop=(mc == 1))
            k.tt("dve", outc[hh][:, off:off + ln], pso[:, 0:ln], rden[:, 0:ln], ALU.mult)

    def layernorm(li, tiles, ln_out_bf, final_out=None, tok0=0):
        gcol = lambda kc: lnp_t[:, 2 * li, kc:kc + 1]
        bcol = lambda kc: lnp_t[:, 2 * li + 1, kc:kc + 1]
        for (off, ln) in tiles:
            ps_s = nxt(pss, pi)
            ps_q = nxt(pss, pi)
            for kc in range(KC):
                tb = nxt(tmpb, tbi)
                k.copy("act", tb[:, 0:ln], resid[kc][:, off:off + ln])
                tq = nxt(tmpb, tbi)
                k.act(tq[:, 0:ln], resid[kc][:, off:off + ln], AF.Square)
                k.mm(ps_s[:, 0:ln], ones_b, tb[:, 0:ln], start=(kc == 0), stop=(kc == KC - 1))
                k.mm(ps_q[:, 0:ln], ones_b, tq[:, 0:ln], start=(kc == 0), stop=(kc == KC - 1))
            k.ts("dve", mean_bc[:, 0:ln], ps_s[:, 0:ln], 1.0 / D, None, ALU.mult)
            msq = nxt(tmpf, tfi)
            k.tt("dve", msq[:, 0:ln], mean_bc[:, 0:ln], mean_bc[:, 0:ln], ALU.mult)
            var = nxt(tmpf, tfi)
            k.stt("dve", var[:, 0:ln], ps_q[:, 0:ln], 1.0 / D, msq[:, 0:ln], ALU.mult, ALU.subtract)
            k.ts("dve", var[:, 0:ln], var[:, 0:ln], LN_EPS, None, ALU.add)
            k.act(var[:, 0:ln], var[:, 0:ln], AF.Sqrt)
            k.op("dve", lambda h, o=rstd_bc[:, 0:ln], i=var[:, 0:ln]: h.reciprocal(o.ap, i.ap), reads=[var], writes=[rstd_bc])
            for kc in range(KC):
                t1 = nxt(tmpf, tfi)
                k.tt("dve", t1[:, 0:ln], resid[kc][:, off:off + ln], mean_bc[:, 0:ln], ALU.subtract)
                k.tt("dve", t1[:, 0:ln], t1[:, 0:ln], rstd_bc[:, 0:ln], ALU.mult)
                if final_out is None:
                    k.act(resid[kc][:, off:off + ln], t1[:, 0:ln], AF.Identity, bias=bcol(kc), scale=gcol(kc))
                    k.act(ln_out_bf[kc][:, off:off + ln], t1[:, 0:ln], AF.Identity, bias=bcol(kc), scale=gcol(kc))
                else:
                    ys = nxt(yst, ysi)
                    k.act(ys[:, 0:ln], t1[:, 0:ln], AF.Identity, bias=bcol(kc), scale=gcol(kc))
                    k.dma("sp", final_out[kc * 128:(kc + 1) * 128, tok0 + off:tok0 + off + ln], ys.ap[:, 0:ln], reads=[ys])

    sts = []
    t0 = 0
    while t0 < NP:
        n = min(ST, NP - t0)
        sts.append([t0, n, 0])
        t0 += n
    sts[-1][2] = NS
    for (t0, n, ns) in sts:
        TTn = n + ns
        tiles = [(o, min(512, n - o)) for o in range(0, n, 512)]
        if ns:
            tiles.append((n, ns))
        def dram_cols(ap2d, r0, nr):
            prs = [(0, ap2d[r0:r0 + nr, t0:t0 + n])]
            if ns:
                prs.append((n, ap2d[r0:r0 + nr, NP:NP + ns]))
            return prs
        for kc in range(KC):
            k.dma("pool", [(xin[kc].ap[:, o:o + a.shape[1]], a) for (o, a) in dram_cols(xT, kc * 128, 128)], None, writes=[xin[kc]])
        if G is None:
            for kc in range(KC):
                k.dma("sp", [(hmi[kc].ap[:, o:o + a.shape[1]], a) for (o, a) in dram_cols(hmT, kc * 128, 128)], None, writes=[hmi[kc]])
            for c in range(4):
                k.dma("sp", [(atc[c].ap[:, o:o + a.shape[1]], a) for (o, a) in dram_cols(attT, c * 128, 128)], None, writes=[atc[c]])
        else:
            for r in range(4):
                for kk in range(5):
                    dstt = atc[r] if kk == 0 else hmi[r * 4 + kk - 1]
                    for (off, ln) in tiles:
                        ps = nxt(pss, pi)
                        if off < n:
                            cd = nxt(cands, cdi)
                            prs, gb = [], []
                            for blk in range(4):
                                gt, gbuf = G["g"][(kk, blk // 2)]
                                c0_ = (blk % 2) * NP + t0 + off
                                prs.append((cd.ap[:, blk, 0:ln], gt[r * 128:(r + 1) * 128, c0_:c0_ + ln]))
                                gb.append(gbuf)
                            k.dma("sp", prs, None, reads=gb, writes=[cd])
                            for blk in range(4):
                                k.mm(ps[:, 0:ln], selq_t[:, blk, :], cd[:, blk, 0:ln], start=(blk == 0), stop=(blk == 3))
                        else:
                            cd = nxt(cands8, cd8i)
                            gt, gbuf = G["gs"]
                            k.dma("sp", cd.ap[:, :, 0:ln], gt[r * 128:(r + 1) * 128, kk * 32:(kk + 1) * 32].rearrange("p (j s) -> p j s", s=ln),
                                  reads=[gbuf], writes=[cd])
                            for j in range(8):
                                k.mm(ps[:, 0:ln], sels_t[:, j, :], cd[:, j, 0:ln], start=(j == 0), stop=(j == 7))
                        k.copy("act", dstt[:, off:off + ln], ps[:, 0:ln])
        for oc in range(KC):
            w_ga = load_w(wg, 0, KC, oc * 128)
            w_gm = load_w(wg, 0, KC, D + oc * 128)
            w_a = load_w(w_br_a, 0, 4, oc * 128)
            w_m = load_w(w_br_m, 0, KC, oc * 128)
            for (off, ln) in tiles:
                p_ga = group(w_ga, KC, xin, off, ln)
                p_a = group(w_a, 4, atc, off, ln)
                s1 = nxt(tmpf, tfi)
                k.act(s1[:, 0:ln], p_ga[:, 0:ln], AF.Sigmoid)
                k.tt("dve", s1[:, 0:ln], s1[:, 0:ln], p_a[:, 0:ln], ALU.mult)
                p_gm = group(w_gm, KC, xin, off, ln)
                p_m = group(w_m, KC, hmi, off, ln)
                s2 = nxt(tmpf, tfi)
                k.act(s2[:, 0:ln], p_gm[:, 0:ln], AF.Sigmoid)
                k.tt("dve", s2[:, 0:ln], s2[:, 0:ln], p_m[:, 0:ln], ALU.mult)
                k.tt("dve", mg[oc][:, off:off + ln], s1[:, 0:ln], s2[:, 0:ln], ALU.add)
        for oc in range(KC):
            ws = load_w(w_mix, 0, KC, oc * 128)
            for (off, ln) in tiles:
                xs = nxt(xst, xsi)
                src = xT[oc * 128:(oc + 1) * 128, t0 + off:t0 + off + ln] if off < n else xT[oc * 128:(oc + 1) * 128, NP:NP + ns]
                k.dma("sp", xs.ap[:, 0:ln], src, writes=[xs])
                ps = group(ws, KC, mg, off, ln)
                k.stt("dve", resid[oc][:, off:off + ln], xs[:, 0:ln], ALPHA, ps[:, 0:ln], ALU.mult, ALU.add)
        layernorm(0, tiles, xbf)
        for hc in range(4):
            ws = load_w(w_cq, 0, KC, hc * 128)
            for (off, ln) in tiles:
                ps = group(ws, KC, xbf, off, ln)
                k.copy("act", atc[hc][:, off:off + ln], ps[:, 0:ln])
        for (off, ln) in tiles:
            if off < n:
                cross_attn(atc, kT_b, v_b, occ, off, ln)
            else:
                for s in range(ns):
                    for mc in range(2):
                        k.dma("sp", skv_f.ap, skv[s, mc * 128:(mc + 1) * 128, :, :], writes=[skv_f])
                        k.copy("act", sv_b[:, mc, :], skv_f[:, 1, :])
                        for hh in range(4):
                            psx = nxt(pss, pi)
                            k.transpose(psx[:, 0:128], skv_f[:, 0, hh * 128:(hh + 1) * 128], ident_f)
                            k.copy("dve", skT_b[:, hh, mc * 128:(mc + 1) * 128], psx[:, 0:128])
                    cross_attn(atc, skT_b, sv_b, occ, off + s, 1)
        for oc in range(KC):
            ws = load_w(w_co, 0, 4, oc * 128)
            for (off, ln) in tiles:
                ps = group(ws, 4, occ, off, ln)
                k.stt("dve", resid[oc][:, off:off + ln], resid[oc][:, off:off + ln], ALPHA, ps[:, 0:ln], ALU.mult, ALU.add)
        layernorm(1, tiles, xbf)
        for oc in range(KC):
            for (off, ln) in tiles:
                k.ts("dve", resid[oc][:, off:off + ln], resid[oc][:, off:off + ln], ALPHA, None, ALU.mult)
        for g in range(FC // 8):
            for j in range(8):
                ws = load_w(w_up, 0, KC, (g * 8 + j) * 128)
                for (off, ln) in tiles:
                    ps = group(ws, KC, xbf, off, ln)
                    r1 = nxt(tmpf, tfi)
                    k.act(r1[:, 0:ln], ps[:, 0:ln], AF.Relu)
                    k.tt("dve", hg[j][:, off:off + ln], r1[:, 0:ln], r1[:, 0:ln], ALU.mult)
            for oc in range(KC):
                ws = load_w(w_down, g * 8 * 128, 8, oc * 128)
                for (off, ln) in tiles:
                    ps = group(ws, 8, hg, off, ln)
                    k.tt("dve", resid[oc][:, off:off + ln], resid[oc][:, off:off + ln], ps[:, 0:ln], ALU.add)
        ptiles = [(o, l) for (o, l) in tiles if o < n]
        layernorm(2, ptiles, None, final_out=yT, tok0=t0)
        if ns:
            layernorm(2, [(n, ns)], None, final_out=yT, tok0=NP - n)
    k.wait_all("sp", xst)


D = 2048
KC = 16
GROUPS = ((128, 1), (512, 4), (2048, 16))
SCALE = 128.0 ** -0.5
LN_EPS = 1e-5
NEG = -1.0e30


def make_shared(k, cm):
    sh = {}
    sh["pss"] = [k.ptile([128, 512], F32, f"ps{i}") for i in range(6)]
    sh["pi"] = [0]
    sh["pst"] = [k.ptile([128, 1024], BF16, f"pst{i}") for i in range(2)]
    sh["pti"] = [0]
    cmt = k.tile([128, 6, 128], F32, "cmt")
    k.dma("sp", cmt.ap, cm, writes=[cmt])
    sh["cmt"] = cmt
    ident_b = k.tile([128, 128], BF16, "identb")
    k.copy("dve", ident_b, cmt[:, 0, :])
    mask_cur = k.tile([128, 128], BF16, "mcur")
    k.copy("dve", mask_cur, cmt[:, 1, :])
    mask_prev = k.tile([128, 128], BF16, "mprev")
    k.copy("dve", mask_prev, cmt[:, 2, :])
    ones_b = k.tile([128, 128], BF16, "onesb")
    k.memset("dve", ones_b, 1.0)
    ones_f = k.tile([128, 128], F32, "onesf")
    k.memset("dve", ones_f, 1.0)
    negI_b = k.tile([128, 128], BF16, "negIb")
    k.ts("dve", negI_b, cmt[:, 0, :], -1.0, None, ALU.mult)
    posm_b = k.tile([128, 128], BF16, "posmb")
    k.copy("dve", posm_b, cmt[:, 5, :])
    negm_b = k.tile([128, 128], BF16, "negmb")
    k.copy("dve", negm_b, cmt[:, 3, :])
    sh.update(ident_b=ident_b, mask_cur=mask_cur, mask_prev=mask_prev, ones_b=ones_b, ones_f=ones_f,
              negI_b=negI_b, posm_b=posm_b, negm_b=negm_b)
    return sh


def l1_io(nc, part, T_, NSMP, ST, io=None):
    io = {} if io is None else io
    NTT = T_ + NSMP

    def din(name, shape, dt=F32):
        if name not in io:
            io[name] = nc.dram_tensor(name, list(shape), dt, kind="ExternalInput").ap()

    def dout(name, shape, dt=F32):
        io[name] = nc.dram_tensor(name, list(shape), dt, kind="ExternalOutput").ap()

    din("xT", [D, NTT])
    din("cm", [128, 6, 128])
    if part == "A":
        din("wA", [D, 1152])
        din("cKV", [NSMP, 128, 3, 2, 128])
        dout("kvT_o", [3, 2, 128, ST + NSMP])
    else:
        din("wB", [D, 1538])
        din("cst", [128, 4, 3, NSMP])
        din("c0", [NSMP, 256, 512])
        din("n0", [128, 2, NSMP])
        din("m0", [128, NSMP])
        din("cpar", [128, 4, 5])
        din("gpar", [128, 2])
        din("gnorm", [128, 512])
        dout("conv_o", [128, 4, 3])
        dout("sconv_o", [128, 4, 3, NSMP])
        dout("pc_o", [256, 512])
        dout("pn_o", [128, 2])
        dout("pm_o", [128, 1])
        dout("sc_o", [NSMP, 256, 512])
        dout("sn_o", [128, 2, NSMP])
        dout("sm_o", [128, NSMP])
    return io


def build_l1(part, T_=8192, NSMP=32, ST=2048):
    nc = bass.Bass("TRN2", target_bir_lowering=False)
    io = l1_io(nc, part, T_, NSMP, ST)
    NTT = T_ + NSMP
    if part == "A":
        o = nc.dram_tensor("attT_o", [128, NTT], BF16, kind="ExternalOutput").ap()
        io["att_dst"] = lambda c0, n: [(o[:, c0:c0 + n], 0, n, [])]
    else:
        o = nc.dram_tensor("hmT_o", [512, NTT], BF16, kind="ExternalOutput").ap()
        io["hm_dst"] = lambda c, c0, n: [(o[c * 128:(c + 1) * 128, c0:c0 + n], 0, n, [])]
    k = KB(nc)
    sh = make_shared(k, io["cm"])
    emit_l1(k, part, io, sh, T_, NSMP, ST)
    k.emit()
    return nc


def emit_l1(k, part, io, sh, T_=8192, NSMP=32, ST=2048):
    NTT = T_ + NSMP
    NST = T_ // ST
    xT = io["xT"]
    if part == "A":
        wA, cKV, kvT_o, att_dst = io["wA"], io["cKV"], io["kvT_o"], io["att_dst"]
    else:
        wB, cst, c0, n0, m0, cpar, gpar, gnorm = (io[n_] for n_ in ("wB", "cst", "c0", "n0", "m0", "cpar", "gpar", "gnorm"))
        conv_o, sconv_o, pc_o, pn_o, pm_o, sc_o, sn_o, sm_o = (io[n_] for n_ in ("conv_o", "sconv_o", "pc_o", "pn_o", "pm_o", "sc_o", "sn_o", "sm_o"))
        hm_dst = io["hm_dst"]

    def nxt(lst, ctr):
        t = lst[ctr[0] % len(lst)]
        ctr[0] += 1
        return t

    pss, pi, pst, pti = sh["pss"], sh["pi"], sh["pst"], sh["pti"]
    cmt = sh["cmt"]
    ident_f = cmt[:, 0, :]
    tri_f = cmt[:, 1, :]
    negm_f = cmt[:, 3, :]
    posm_f = cmt[:, 5, :]
    padneg = cmt[:, 4, 0:1]
    padone = cmt[:, 4, 1:2]
    ident_b, mask_cur, mask_prev, ones_b, ones_f = (sh[n_] for n_ in ("ident_b", "mask_cur", "mask_prev", "ones_b", "ones_f"))
    xts = [k.tile([128, KC, 512], BF16, f"xt{part}{i}") for i in range(2)]
    xi = [0]
    if part == "A":
        ostf = [k.tile([128, 512], F32, f"ostf{part}{i}") for i in range(3)]
        ofi = [0]
        tmpb = [k.tile([128, 128], BF16, f"tb{part}{i}") for i in range(6)]
        tbi = [0]

    def load_x(t0, n):
        xt = nxt(xts, xi)
        k.dma("pool", [(xt.ap[:, kc, 0:n], xT[kc * 128:(kc + 1) * 128, t0:t0 + n]) for kc in range(KC)], None, writes=[xt])
        return xt

    tiles = [(t0, 512) for t0 in range(0, T_, 512)] + [(T_, NSMP)]
    xpre = {}
    n_pref = len(tiles) if part == "A" else 0

    def get_x(idx):
        if idx not in xpre:
            xpre[idx] = load_x(*tiles[idx])
        xt = xpre.pop(idx)
        if idx + 1 < n_pref and (idx + 1) not in xpre:
            xpre[idx + 1] = load_x(*tiles[idx + 1])
        return xt

    if part == "A":
        wAb = k.tile([128, KC, 1152], BF16, "wAb")
        k.dma("pool", [(wAb.ap[:, kc, :], wA[kc * 128:(kc + 1) * 128, :]) for kc in range(KC)], None, writes=[wAb])
        W2 = 2 * ST
        qT = [k.tile([128, ST], BF16, f"qT{g}") for g in range(3)]
        kT = [k.tile([128, W2], BF16, f"kT{g}") for g in range(3)]
        vT = [k.tile([128, W2], BF16, f"vT{g}") for g in range(3)]
        numA = k.tile([128, ST], F32, "numA")
        denA = k.tile([128, ST], F32, "denA")
        atto = k.tile([128, ST], BF16, "atto")
        Vb = [k.tile([128, 128], BF16, f"Vb{i}") for i in range(2)]
        sq = [k.tile([128, NSMP], BF16, f"sq{g}") for g in range(3)]
        sk = [k.tile([128, NSMP], BF16, f"sk{g}") for g in range(3)]
        sv = [k.tile([128, NSMP], BF16, f"sv{g}") for g in range(3)]
        ckv = [k.tile([128, 3, 2, 128], F32, f"ckv{i}") for i in range(2)]
        cki = [0]
        cKT_b = k.tile([128, 128], BF16, "cKTb")
        cV_b = k.tile([128, 128], BF16, "cVb")
        numS = k.tile([128, NSMP], F32, "numS")
        denS = k.tile([128, NSMP], F32, "denS")

        def proj_A(xt, n, dst_fn, kv_out_col=None):
            for blk in range(9):
                ps = nxt(pss, pi)
                for kc in range(KC):
                    k.mm(ps[:, 0:n], wAb[:, kc, blk * 128:(blk + 1) * 128], xt[:, kc, 0:n], start=(kc == 0), stop=(kc == KC - 1))
                k.copy("act", dst_fn(blk), ps[:, 0:n])
                if kv_out_col is not None and blk >= 3:
                    of = nxt(ostf, ofi)
                    k.copy("dve", of[:, 0:n], ps[:, 0:n])
                    g, kv = (blk - 3) % 3, (blk - 3) // 3
                    k.dma("sp", kvT_o[g, kv, :, kv_out_col:kv_out_col + n], of.ap[:, 0:n], reads=[of])

        def att_unit(q_ap, kblocks, acc_num, acc_den, first):
            nq = q_ap.ap.shape[1]
            es = []
            for (kt, vb, mask) in kblocks:
                ps = nxt(pss, pi)
                if mask is None:
                    k.mm(ps[:, 0:nq], kt, q_ap)
                else:
                    k.mm(ps[:, 0:nq], kt, q_ap, start=True, stop=False)
                    if mask is mask_cur:
                        k.mm(ps[:, 0:nq], sh["negI_b"], sh["posm_b"][:, 0:nq], start=False, stop=True)
                    else:
                        k.mm(ps[:, 0:nq], ident_b, sh["negm_b"][:, 0:nq], start=False, stop=True)
                e = nxt(tmpb, tbi)
                k.act(e[:, 0:nq], ps[:, 0:nq], AF.Exp, scale=SCALE)
                es.append(e)
            psn = nxt(pss, pi)
            psd = nxt(pss, pi)
            nb = len(kblocks)
            for i, (kt, vb, mask) in enumerate(kblocks):
                k.mm(psn[:, 0:nq], vb, es[i][:, 0:nq], start=(i == 0), stop=(i == nb - 1))
            for i in range(nb):
                k.mm(psd[:, 0:nq], ones_b, es[i][:, 0:nq], start=(i == 0), stop=(i == nb - 1))
            if first:
                k.copy("dve", acc_num, psn[:, 0:nq])
                k.copy("act", acc_den, psd[:, 0:nq])
            else:
                k.tt("dve", acc_num, acc_num, psn[:, 0:nq], ALU.add)
                k.tt("dve", acc_den, acc_den, psd[:, 0:nq], ALU.add)

        def vtrans(src_ap, dst):
            pt = nxt(pst, pti)
            k.transpose(pt[:, 0:128], src_ap, ident_b)
            k.copy("dve", dst, pt[:, 0:128])

        for J in range(NST):
            cur = (J % 2) * ST
            prv = ((J + 1) % 2) * ST
            last = (J == NST - 1)
            for jj in range(ST // 512):
                t0 = J * ST + jj * 512
                xt = get_x(t0 // 512)
                o = jj * 512

                def dst(blk, o=o):
                    g, kind = blk % 3, blk // 3
                    if kind == 0:
                        return qT[g][:, o:o + 512]
                    return (kT if kind == 1 else vT)[g][:, cur + o:cur + o + 512]
                proj_A(xt, 512, dst, kv_out_col=(o if last else None))
            for g, (win, d) in enumerate(GROUPS):
                nb = ST // (128 * d)
                for r in range(d):
                    vprev = None
                    for n in range(nb):
                        sl = slice(r + d * 128 * n, r + d * 128 * n + d * 127 + 1, d)
                        slc = slice(cur + sl.start, cur + sl.stop, d)
                        blocks = []
                        if n == 0:
                            if J > 0:
                                slp = slice(prv + r + d * 128 * (nb - 1), prv + r + d * 128 * (nb - 1) + d * 127 + 1, d)
                                vprev = nxt(Vb, [n + 1])
                                vtrans(vT[g][:, slp], vprev)
                                blocks.append((kT[g][:, slp], vprev, mask_prev))
                        else:
                            slp = slice(cur + r + d * 128 * (n - 1), cur + r + d * 128 * (n - 1) + d * 127 + 1, d)
                            blocks.append((kT[g][:, slp], vprev, mask_prev))
                        vcur = Vb[n % 2]
                        vtrans(vT[g][:, slc], vcur)
                        blocks.append((kT[g][:, slc], vcur, mask_cur))
                        att_unit(qT[g][:, sl], blocks, numA[:, sl], denA[:, sl], first=(g == 0))
                        vprev = vcur
            k.op("dve", lambda h: h.reciprocal(denA.ap, denA.ap), reads=[denA], writes=[denA])
            k.tt("dve", atto, numA, denA, ALU.mult)
            for (dst, so, nn, dbufs) in att_dst(J * ST, ST):
                k.dma("sp", dst, atto.ap[:, so:so + nn], reads=[atto], writes=dbufs)
        xt = get_x(len(tiles) - 1)

        def dsts(blk):
            g, kind = blk % 3, blk // 3
            return (sq, sk, sv)[kind][g]
        proj_A(xt, NSMP, dsts, kv_out_col=ST)
        for g in range(3):
            pr = nxt(tmpb, tbi)
            k.tt("dve", pr[:, 0:NSMP], sq[g], sk[g], ALU.mult)
            ps = nxt(pss, pi)
            k.mm(ps[:, 0:NSMP], ones_b, pr[:, 0:NSMP])
            ef = nxt(ostf, ofi)
            k.act(ef[:, 0:NSMP], ps[:, 0:NSMP], AF.Exp, scale=SCALE)
            if g == 0:
                k.copy("dve", denS, ef[:, 0:NSMP])
                k.tt("dve", numS, ef[:, 0:NSMP], sv[g], ALU.mult)
            else:
                k.tt("dve", denS, denS, ef[:, 0:NSMP], ALU.add)
                t2 = nxt(ostf, ofi)
                k.tt("dve", t2[:, 0:NSMP], ef[:, 0:NSMP], sv[g], ALU.mult)
                k.tt("dve", numS, numS, t2[:, 0:NSMP], ALU.add)
        for s in range(NSMP):
            cb = nxt(ckv, cki)
            k.dma("sp", cb.ap, cKV[s], writes=[cb])
            for g in range(3):
                psx = nxt(pss, pi)
                k.transpose(psx[:, 0:128], cb[:, g, 0, :], ident_f)
                k.copy("dve", cKT_b, psx[:, 0:128])
                k.copy("act", cV_b, cb[:, g, 1, :])
                att_unit(sq[g][:, s:s + 1], [(cKT_b, cV_b, None)], numS[:, s:s + 1], denS[:, s:s + 1], first=False)
        k.op("dve", lambda h: h.reciprocal(denS.ap, denS.ap), reads=[denS], writes=[denS])
        k.tt("dve", atto[:, 0:NSMP], numS, denS, ALU.mult)
        for (dst, so, nn, dbufs) in att_dst(T_, NSMP):
            k.dma("sp", dst, atto.ap[:, so:so + nn], reads=[atto], writes=dbufs)
        k.wait_all("sp", [atto] + ostf)
        return

    wBb = k.tile([128, KC, 1538], BF16, "wBb")
    k.dma("pool", [(wBb.ap[:, kc, :], wB[kc * 128:(kc + 1) * 128, :]) for kc in range(KC)], None, writes=[wBb])
    cpar_t = k.tile([128, 4, 5], F32, "cpar_t")
    k.dma("sp", cpar_t.ap, cpar, writes=[cpar_t])
    gpar_t = k.tile([128, 2], F32, "gpar_t")
    k.dma("sp", gpar_t.ap, gpar, writes=[gpar_t])
    gn_t = k.tile([128, 512], F32, "gn_t")
    k.dma("sp", gn_t.ap, gnorm, writes=[gn_t])
    ext = k.tile([128, 4, 3 + 512], F32, "ext")
    k.memset("dve", ext, 0.0)
    qkb = k.tile([128, 4, 512], BF16, "qkb")
    cacc = k.tile([128, 512], F32, "cacc")
    def mkstate(pfx):
        return {"Cst": k.tile([128, 2, 512], F32, "Cst" + pfx), "Cbf": k.tile([128, 2, 512], BF16, "Cbf" + pfx),
                "nst": k.tile([128, 2], F32, "nst" + pfx), "nbf": k.tile([128, 2], BF16, "nbf" + pfx),
                "mst": k.tile([128, 1], F32, "mst" + pfx)}
    cur = {"pfx": "P", "pss": pss[0:1], "pi": [0], "pst": pst[0:1], "pti": [0]}

    def nps():
        return nxt(cur["pss"], cur["pi"])

    def npt():
        return nxt(cur["pst"], cur["pti"])
    hmst = [k.tile([128, 4, 512], BF16, f"hmst{i}") for i in range(2)]
    hmi = [0]
    sm = {}

    def small(name, shape, dt=F32):
        key = (cur["pfx"], name, tuple(shape), dt)
        if key not in sm:
            sm[key] = [k.tile(shape, dt, f"s_{cur['pfx']}_{name}_{i}") for i in range(3 if name == "sotok" else 2)], [0]
        lst, ctr = sm[key]
        return nxt(lst, ctr)

    def bc_row(col, name):
        bcm = small(name + "_bc", [128, 128])
        k.ts("dve", bcm, ones_f, col, None, ALU.mult)
        ps = nps()
        k.mm(ps[:, 0:128], bcm, ident_f)
        return ps

    def stage_P(job):
        if job.get("pre_P"):
            yield from job["pre_P"]()
        xc, kT_c, pad, cx = job["xc"], job["kT_c"], job["pad"], job
        psv = nps()
        for kc in range(KC):
            k.mm(psv, xc[:, kc, :], wBb[:, kc, 512:1024], start=(kc == 0), stop=(kc == KC - 1))
        v_tok = small("vtok", [128, 512], BF16)
        k.copy("act", v_tok, psv)
        yield
        pso = nps()
        for kc in range(KC):
            k.mm(pso, xc[:, kc, :], wBb[:, kc, 1024:1536], start=(kc == 0), stop=(kc == KC - 1))
        so_tok = small("sotok", [128, 512], F32)
        k.act(so_tok, pso, AF.Sigmoid)
        yield
        psg = nps()
        for kc in range(KC):
            k.mm(psg[:, 0:2], xc[:, kc, :], wBb[:, kc, 1536:1538], start=(kc == 0), stop=(kc == KC - 1))
        ig = small("ig", [128, 1])
        k.tt("dve", ig, psg[:, 0:1], gpar_t[:, 0:1], ALU.add)
        z = small("z", [128, 1])
        k.tt("dve", z, psg[:, 1:2], gpar_t[:, 1:2], ALU.add)
        yield
        k.act(z, z, AF.Exp, scale=-1.0)
        k.act(z, z, AF.Ln, bias=1.0)
        lf = small("lf", [128, 1])
        k.ts("dve", lf, z, -1.0, None, ALU.mult)
        if pad:
            k.tt("dve", ig, ig, padneg, ALU.add)
            k.tt("dve", lf, lf, padone, ALU.mult)
        yield
        k_tok = small("ktok", [128, 256], BF16)
        pt = npt()
        for j in range(2):
            k.transpose(pt[:, j * 128:(j + 1) * 128], kT_c[j], ident_b)
        k.copy("dve", k_tok, pt[:, 0:256])
        yield
        psb = nps()
        k.mm(psb[:, 0:1], tri_f, lf)
        b_col = small("bcol", [128, 1])
        k.copy("dve", b_col, psb[:, 0:1])
        lfb = small("lfb", [128, 128])
        k.ts("dve", lfb, ones_f, lf, None, ALU.mult)
        yield
        psbr = nps()
        k.mm(psbr[:, 0:128], lfb, tri_f)
        bL = small("bL", [128, 1])
        k.copy("dve", bL, psbr[:, 127:128])
        a_col = small("acol", [128, 1])
        k.tt("dve", a_col, ig, b_col, ALU.subtract)
        yield
        psa = bc_row(a_col, "a")
        amax = small("amax", [128, 1])
        k.reduce("dve", amax, psa[:, 0:128], ALU.max)
        am = small("am", [128, 128])
        k.tt("dve", am, psa[:, 0:128], negm_f, ALU.add)
        cmx = small("cmx", [128, 1])
        k.reduce("dve", cmx, am, ALU.max)
        yield
        pscm = bc_row(cmx, "cm")
        cmrow = small("cmrow", [128, 128])
        k.copy("act", cmrow, pscm[:, 0:128])
        cmask = small("cmask", [128, 128])
        k.tt("dve", cmask, cmrow, posm_f, ALU.add)
        yield
        psqk = nps()
        for j in range(2):
            k.mm(psqk[:, 0:128], kT_c[j], job["qT_c"][j], start=(j == 0), stop=(j == 1))
        qk_sb = small("qksb", [128, 128])
        k.copy("act", qk_sb, psqk[:, 0:128])
        cx.update(v_tok=v_tok, so_tok=so_tok, k_tok=k_tok, b_col=b_col, bL=bL, a_col=a_col, amax=amax, cmx=cmx,
                  cmrow=cmrow, cmask=cmask, qk_sb=qk_sb)

    def stage_F(job):
        if job.get("pre_F"):
            yield from job["pre_F"]()
        stt_ = job["st"]
        Cst, Cbf, nst, nbf, mst = (stt_[n_] for n_ in ("Cst", "Cbf", "nst", "nbf", "mst"))
        qT_c, kT_c = job["qT_c"], job["kT_c"]
        v_tok, k_tok, b_col, bL, a_col, amax, cmx = (job[n_] for n_ in ("v_tok", "k_tok", "b_col", "bL", "a_col", "amax", "cmx"))
        cmrow, cmask, qk_sb = job["cmrow"], job["cmask"], job["qk_sb"]
        rn1 = small("rn1", [128, 128])
        k.ts("dve", rn1, cmask, mst, None, ALU.max)
        rn2 = small("rn2", [128, 128])
        k.ts("dve", rn2, cmrow, mst, None, ALU.max)
        mx = small("mx", [128, 1])
        k.tt("dve", mx, cmx, mst, ALU.max)
        m_col = small("mcol", [128, 1])
        k.tt("dve", m_col, b_col, mx, ALU.add)
        yield
        dT = small("dT", [128, 128])
        k.act(dT, rn1, AF.Exp, bias=a_col, scale=-1.0)
        wrow = small("wrow", [128, 128])
        k.act(wrow, rn2, AF.Exp, bias=mst, scale=-1.0)
        yield
        S_bf = small("Sbf", [128, 128], BF16)
        k.tt("dve", S_bf, qk_sb, dT, ALU.mult)
        qw = small("qw", [128, 2, 128], BF16)
        for j in range(2):
            k.tt("dve", qw[:, j, :], qT_c[j], wrow, ALU.mult)
        yield
        psn = nps()
        k.mm(psn, S_bf, v_tok, start=True, stop=False)
        for j in range(2):
            k.mm(psn, qw[:, j, :], Cbf[:, j, :], start=False, stop=(j == 1))
        psd = nps()
        k.mm(psd[:, 0:1], S_bf, ones_b[:, 0:1], start=True, stop=False)
        for j in range(2):
            k.mm(psd[:, 0:1], qw[:, j, :], nbf[:, j:j + 1], start=False, stop=(j == 1))
        t1 = small("t1", [128, 1])
        k.tt("dve", t1, amax, mst, ALU.max)
        mnew = small("mnew", [128, 1])
        k.tt("dve", mnew, bL, t1, ALU.add)
        bm = small("bm", [128, 1])
        k.tt("dve", bm, bL, mnew, ALU.subtract)
        yield
        wk = small("wk", [128, 1])
        k.act(wk, a_col, AF.Exp, bias=bm)
        dec = small("dec", [128, 1])
        k.tt("dve", dec, bm, mst, ALU.add)
        k.act(dec, dec, AF.Exp)
        kw = small("kw", [128, 256], BF16)
        k.ts("dve", kw, k_tok, wk, None, ALU.mult)
        dn = small("dn", [128, 1])
        k.act(dn, psd[:, 0:1], AF.Abs)
        em = small("em", [128, 1])
        k.act(em, m_col, AF.Exp, scale=-1.0)
        k.tt("dve", dn, dn, em, ALU.max)
        k.op("dve", lambda h, o=dn: h.reciprocal(o.ap, o.ap), reads=[dn], writes=[dn])
        hh = small("hh", [128, 512])
        k.ts("dve", hh, psn, dn, None, ALU.mult)
        job["hh"] = hh
        yield
        for j in range(2):
            psc = nps()
            k.mm(psc, kw[:, j * 128:(j + 1) * 128], v_tok)
            k.stt("dve", Cst[:, j, :], Cst[:, j, :], dec, psc, ALU.mult, ALU.add)
            k.copy("act", Cbf[:, j, :], Cst[:, j, :])
            psn2 = nps()
            k.mm(psn2[:, 0:1], kw[:, j * 128:(j + 1) * 128], ones_b[:, 0:1])
            k.stt("dve", nst[:, j:j + 1], nst[:, j:j + 1], dec, psn2[:, 0:1], ALU.mult, ALU.add)
            yield
        k.copy("dve", nbf, nst)
        k.copy("dve", mst, mnew)
        if job.get("post_F"):
            yield from job["post_F"]()

    def stage_N(job):
        hh, so_tok, hm_dst = job["hh"], job["so_tok"], job["hm_dst"]
        st6 = small("st6", [128, 6])
        k.op("dve", lambda h, o=st6, i=hh: h.bn_stats(o.ap, i.ap), reads=[hh], writes=[st6])
        mv = small("mv", [128, 2])
        k.op("dve", lambda h, o=mv, i=st6: h.bn_aggr(o.ap, i.ap), reads=[st6], writes=[mv])
        rs = small("rs", [128, 1])
        k.ts("dve", rs, mv[:, 1:2], LN_EPS, None, ALU.add)
        yield
        k.act(rs, rs, AF.Sqrt)
        k.op("dve", lambda h, o=rs: h.reciprocal(o.ap, o.ap), reads=[rs], writes=[rs])
        k.ts("dve", hh, hh, mv[:, 0:1], None, ALU.subtract)
        yield
        k.ts("dve", hh, hh, rs, None, ALU.mult)
        k.tt("pool", hh, hh, gn_t, ALU.mult)
        hb = small("hb", [128, 512], BF16)
        k.tt("pool", hb, hh, so_tok, ALU.mult)
        yield
        pt = npt()
        for c in range(4):
            k.transpose(pt[:, c * 128:(c + 1) * 128], hb[:, c * 128:(c + 1) * 128], ident_b)
        yield
        for c in range(4):
            k.copy("act", hm_dst[:, c, :], pt[:, c * 128:(c + 1) * 128])
        if job.get("post_N"):
            yield from job["post_N"]()

    def conv_silu(n, ext_taps, dst_qk):
        for blk in range(4):
            k.act(cacc[:, 0:n], ext_taps(blk, 0), AF.Identity, bias=cpar_t[:, blk, 4:5], scale=cpar_t[:, blk, 0:1])
            for j in range(1, 4):
                k.stt("dve", cacc[:, 0:n], ext_taps(blk, j), cpar_t[:, blk, j:j + 1], cacc[:, 0:n], ALU.mult, ALU.add)
            if blk < 2:
                k.act(dst_qk(blk), cacc[:, 0:n], AF.Silu)
            else:
                k.act(cacc[:, 0:n], cacc[:, 0:n], AF.Silu)
                k.ts("dve", dst_qk(blk), cacc[:, 0:n], 256.0 ** -0.5, None, ALU.mult)

    def proj_qk(xt, n, dst):
        for blk in range(4):
            ps = nps()
            for kc in range(KC):
                k.mm(ps[:, 0:n], wBb[:, kc, blk * 128:(blk + 1) * 128], xt[:, kc, 0:n], start=(kc == 0), stop=(kc == KC - 1))
            k.copy("act", dst(blk), ps[:, 0:n])

    qkbs = [qkb, k.tile([128, 4, 512], BF16, "qkb1")]
    SP_ = mkstate("P")
    SS2 = [mkstate("S0"), mkstate("S1")]
    xs_t = k.tile([128, KC, NSMP], BF16, "xs_t")
    exts = k.tile([128, 4, 4, NSMP], F32, "exts")
    sqk = k.tile([128, 4, NSMP], BF16, "sqk")
    n0_t = k.tile([128, 2, NSMP], F32, "n0_t")
    m0_t = k.tile([128, NSMP], F32, "m0_t")
    sn_t = k.tile([128, 2, NSMP], F32, "sn_t")
    sm_t = k.tile([128, NSMP], F32, "sm_t")
    xpad = k.tile([128, KC, 128], BF16, "xpad")
    qkpads = [k.tile([128, 4, 128], BF16, f"qkpad{i}") for i in range(2)]
    hmss = [k.tile([128, 4, 128], BF16, f"hms{i}") for i in range(2)]
    hmso = k.tile([128, 4, NSMP], BF16, "hmso")
    for nm in ("Cst", "Cbf", "nst", "nbf", "mst"):
        k.memset("dve", SP_[nm], 0.0)
    jobs = []
    for ti, (t0, n) in enumerate(tiles[:-1]):
        qb = qkbs[ti % 2]
        hst = hmst[ti % 2]
        hold = {}

        def pre_tile(t0=t0, qb=qb, hold=hold):
            xt = get_x(t0 // 512)
            hold["xt"] = xt
            proj_qk(xt, 512, lambda blk: ext[:, blk, 3:3 + 512])
            yield
            conv_silu(512, lambda blk, j: ext[:, blk, j:j + 512], lambda blk: qb[:, blk, :])
            if t0 + 512 == T_:
                k.dma("sp", conv_o, ext.ap[:, :, 512:515], reads=[ext])
            halo = small("halo", [128, 4, 3])
            k.copy("dve", halo, ext[:, :, 512:515])
            k.copy("dve", ext[:, :, 0:3], halo)
            yield

        def post_tile(t0=t0, hst=hst):
            for c in range(4):
                for (dst, so, nn, dbufs) in hm_dst(c, t0, 512):
                    k.dma("sp", dst, hst.ap[:, c, so:so + nn], reads=[hst], writes=dbufs)
            if io.get("after_tile"):
                io["after_tile"](t0)
            yield
        for c in range(4):
            sl = slice(c * 128, (c + 1) * 128)
            job = {"kind": "p", "pad": False, "st": SP_, "sl": sl, "hold": hold,
                   "qT_c": [qb[:, 0, sl], qb[:, 1, sl]], "kT_c": [qb[:, 2, sl], qb[:, 3, sl]], "hm_dst": hst[:, :, sl]}
            if c == 0:
                job["pre_P"] = pre_tile
            if c == 3:
                job["post_N"] = post_tile
            jobs.append(job)

    def final_prompt():
        k.dma("sp", pc_o.rearrange("(j p) v -> p j v", p=128), SP_["Cst"].ap, reads=[SP_["Cst"]])
        k.dma("sp", pn_o, SP_["nst"].ap, reads=[SP_["nst"]])
        k.dma("sp", pm_o, SP_["mst"].ap, reads=[SP_["mst"]])
        yield
    jobs[-1]["post_F"] = final_prompt

    def sample_setup():
        k.dma("pool", [(xs_t.ap[:, kc, :], xT[kc * 128:(kc + 1) * 128, T_:T_ + NSMP]) for kc in range(KC)], None, writes=[xs_t])
        k.dma("sp", exts.ap[:, :, 0:3, :], cst, writes=[exts])
        proj_qk(xs_t, NSMP, lambda blk: exts[:, blk, 3, :])
        yield
        conv_silu(NSMP, lambda blk, j: exts[:, blk, j, :], lambda blk: sqk[:, blk, :])
        k.dma("sp", sconv_o, exts.ap[:, :, 1:4, :], reads=[exts])
        k.dma("sp", n0_t.ap, n0, writes=[n0_t])
        k.dma("sp", m0_t.ap, m0, writes=[m0_t])
        k.memset("dve", xpad, 0.0)
        for qp in qkpads:
            k.memset("dve", qp, 0.0)
        yield
    for s_ in range(NSMP):
        stS = SS2[s_ % 2]
        qp = qkpads[s_ % 2]
        hms = hmss[s_ % 2]

        def pre_s(s_=s_, stS=stS, qp=qp):
            if s_ == 0:
                yield from sample_setup()
            k.dma("sp", stS["Cst"].ap, c0[s_].rearrange("(j p) v -> p j v", p=128), writes=[stS["Cst"]])
            k.copy("act", stS["Cbf"], stS["Cst"])
            k.copy("dve", stS["nst"], n0_t[:, :, s_])
            k.copy("dve", stS["nbf"], stS["nst"])
            k.copy("dve", stS["mst"], m0_t[:, s_:s_ + 1])
            k.copy("dve", xpad[:, :, 0:1], xs_t[:, :, s_:s_ + 1])
            k.copy("dve", qp[:, :, 0:1], sqk[:, :, s_:s_ + 1])
            yield

        def post_Fs(s_=s_, stS=stS):
            k.dma("sp", sc_o[s_].rearrange("(j p) v -> p j v", p=128), stS["Cst"].ap, reads=[stS["Cst"]])
            k.copy("dve", sn_t[:, :, s_], stS["nst"])
            k.copy("dve", sm_t[:, s_:s_ + 1], stS["mst"])
            yield

        def post_Ns(s_=s_, hms=hms):
            k.copy("dve", hmso[:, :, s_:s_ + 1], hms[:, :, 0:1])
            yield
        jobs.append({"kind": "s", "pad": True, "st": stS, "xc": xpad, "qT_c": [qp[:, 0, :], qp[:, 1, :]],
                     "kT_c": [qp[:, 2, :], qp[:, 3, :]], "hm_dst": hms, "pre_P": pre_s, "post_F": post_Fs, "post_N": post_Ns})

    def gen_P(job):
        if job["kind"] == "p":
            def lazy():
                if job.get("pre_P"):
                    yield from job["pre_P"]()
                job["xc"] = job["hold"]["xt"][:, :, job["sl"]]
                job2 = dict(job)
                job2.pop("pre_P", None)
                yield from stage_P(job2)
                for kk_ in ("v_tok", "so_tok", "k_tok", "b_col", "bL", "a_col", "amax", "cmx", "cmrow", "cmask", "qk_sb"):
                    job[kk_] = job2[kk_]
            return lazy()
        return stage_P(job)

    CTX = {"P": {"pfx": "P", "pss": pss[0:1], "pi": [0], "pst": pst[0:1], "pti": [0]},
           "F": {"pfx": "F", "pss": pss[1:3], "pi": [0], "pst": pst[0:1], "pti": [0]},
           "N": {"pfx": "N", "pss": pss[1:3], "pi": [0], "pst": pst[0:1], "pti": [0]},
           "sP": {"pfx": "sP", "pss": pss[3:4], "pi": [0], "pst": pst[1:2], "pti": [0]},
           "sF": {"pfx": "sF", "pss": pss[4:6], "pi": [0], "pst": pst[1:2], "pti": [0]},
           "sN": {"pfx": "sN", "pss": pss[4:6], "pi": [0], "pst": pst[1:2], "pti": [0]}}

    def advance(g, cname):
        cur.clear()
        cur.update(CTX[cname])
        try:
            next(g)
            return True
        except StopIteration:
            return False

    def stage_gens(jl, step, names):
        gens = []
        if 0 <= step - 2 < len(jl):
            gens.append([stage_N(jl[step - 2]), names[2]])
        if 0 <= step - 1 < len(jl):
            gens.append([stage_F(jl[step - 1]), names[1]])
        if step < len(jl):
            gens.append([gen_P(jl[step]), names[0]])
        return gens
    pjobs = [j for j in jobs if j["kind"] == "p"]
    sjobs = [j for j in jobs if j["kind"] == "s"]
    ratio = max(1, len(pjobs) // max(1, len(sjobs)))
    carry = []
    nsteps = max(len(pjobs) + 2, ratio * (len(sjobs) + 2))
    for step in range(nsteps):
        if step % ratio == 0:
            for gc in carry:
                while advance(gc[0], gc[1]):
                    pass
            carry = stage_gens(sjobs, step // ratio, ("sP", "sF", "sN"))
        gens = stage_gens(pjobs, step, ("P", "F", "N"))
        tick = 0
        while gens:
            gens = [gc for gc in gens if advance(gc[0], gc[1])]
            tick += 1
            if tick % ratio == 0:
                carry = [gc for gc in carry if advance(gc[0], gc[1])]
    for gc in carry:
        while advance(gc[0], gc[1]):
            pass
    for c in range(4):
        for (dst, so, nn, dbufs) in hm_dst(c, T_, NSMP):
            k.dma("sp", dst, hmso.ap[:, c, so:so + nn], reads=[hmso], writes=dbufs)
    k.dma("sp", sn_o, sn_t.ap, reads=[sn_t])
    k.dma("sp", sm_o, sm_t.ap, reads=[sm_t])
    k.wait_all("sp", [hmso, sn_t, sm_t, exts, ext] + hmst + [SP_[n_] for n_ in ("Cst", "nst", "mst")]
               + [st_[n_] for st_ in SS2 for n_ in ("Cst", "nst", "mst")])


def build_fused(T_=8192, NSMP=32, DFF=8192, ST1=2048, ST2=1024):
    nc = bass.Bass("TRN2", target_bir_lowering=False)
    NPq = T_ // 4
    HW = T_ // 2
    io = {}
    l1_io(nc, "A", T_, NSMP, ST1, io)
    l1_io(nc, "B", T_, NSMP, ST1, io)
    l2_io(nc, NPq, 4, DFF, io, fused=True)
    k = KB(nc)
    X, Gt = {}, {}
    for kk in range(5):
        for half in range(2):
            X[(kk, half)] = (nc.dram_tensor(f"X{kk}_{half}", [128, HW], BF16), Buf(f"X{kk}_{half}"))
            Gt[(kk, half)] = (nc.dram_tensor(f"G{kk}_{half}", [512, HW], BF16), Buf(f"G{kk}_{half}"))
    XS = (nc.dram_tensor("XS", [128, 5 * NSMP], BF16), Buf("XS"))
    GS = (nc.dram_tensor("GS", [512, 5 * NSMP], BF16), Buf("GS"))

    def xdst(kk, c0, n):
        if c0 >= T_:
            return [(XS[0].ap()[:, kk * NSMP:(kk + 1) * NSMP], 0, n, [XS[1]])]
        out = []
        so = 0
        while n > 0:
            half = c0 // HW
            nn = min(n, (half + 1) * HW - c0)
            xt, xb = X[(kk, half)]
            out.append((xt.ap()[:, c0 - half * HW:c0 - half * HW + nn], so, nn, [xb]))
            c0 += nn
            so += nn
            n -= nn
        return out
    io["att_dst"] = lambda c0, n: xdst(0, c0, n)
    io["hm_dst"] = lambda c, c0, n: xdst(1 + c, c0, n)
    sh = make_shared(k, io["cm"])
    groups = [[0, 1, 2, 3], [4, 5, 6, 7]]
    k.push_scope()
    emit_l1(k, "A", io, sh, T_, NSMP, ST1)
    k.pop_scope()
    for half in range(2):
        k.collective("AllGather", X[(0, half)][0], Gt[(0, half)][0], groups, reads=[X[(0, half)][1]], writes=[Gt[(0, half)][1]])
    def after_tile(tend):
        for half in range(1):
            if tend == (half + 1) * HW:
                for kk in range(1, 5):
                    k.collective("AllGather", X[(kk, half)][0], Gt[(kk, half)][0], groups,
                                 reads=[X[(kk, half)][1]], writes=[Gt[(kk, half)][1]])
    io["after_tile"] = after_tile
    k.push_scope()
    emit_l1(k, "B", io, sh, T_, NSMP, ST1)
    k.pop_scope()
    for kk in range(1, 5):
        k.collective("AllGather", X[(kk, 1)][0], Gt[(kk, 1)][0], groups, reads=[X[(kk, 1)][1]], writes=[Gt[(kk, 1)][1]])
    k.collective("AllGather", XS[0], GS[0], groups, reads=[XS[1]], writes=[GS[1]])
    G = {"g": {key: (t.ap(), b) for key, (t, b) in Gt.items()}, "gs": (GS[0].ap(), GS[1])}
    k.push_scope()
    emit_l2(k, io, sh, NPq, 4, DFF, ST2, G=G)
    k.pop_scope()
    k.emit()
    return nc


GROUPS = ((128, 1), (512, 4), (2048, 16))
NEG = -1.0e30

def const_masks():
    cm = np.zeros((128, 6, 128), np.float32)
    i = np.arange(128)
    cm[:, 0, :] = np.eye(128)
    cm[:, 1, :] = (i[:, None] <= i[None, :])
    cm[:, 2, :] = (i[:, None] >= i[None, :])
    cm[:, 3, :] = np.where(i[None, :] <= i[:, None], 0.0, NEG)
    cm[:, 5, :] = np.where(i[:, None] <= i[None, :], 0.0, -NEG)
    cm[:, 4, 0] = np.where(i == 0, 0.0, NEG)
    cm[:, 4, 1] = (i == 0)
    return cm

def colsA(h):
    c = []
    for base in (0, 1536, 3072):
        for g in range(3):
            c.append(np.arange(base + (4 * g + h) * 128, base + (4 * g + h + 1) * 128))
    return np.concatenate(c)

def colsB(h):
    o = 4608
    mq = np.arange(o + h * 256, o + (h + 1) * 256)
    mk = np.arange(o + 1024 + h * 256, o + 1024 + (h + 1) * 256)
    mv = np.arange(o + 2048 + h * 512, o + 2048 + (h + 1) * 512)
    mo = np.arange(o + 4096 + h * 512, o + 4096 + (h + 1) * 512)
    mi = np.array([o + 6144 + h]); mf = np.array([o + 6148 + h])
    return np.concatenate([mq, mk, mv, mo, mi, mf])

def qkcols(h):
    return np.concatenate([np.arange(h * 256, (h + 1) * 256), np.arange(1024 + h * 256, 1024 + (h + 1) * 256)])

def prep_l1(h, xb, xs, inp):
    w_in = inp["w_in"][0]
    xT = np.ascontiguousarray(np.concatenate([xb, xs], 0).T)
    cm = const_masks()
    cK = []
    for g, (win, d) in enumerate(GROUPS):
        c = inp[f"cache_kv_g{g}"][0]
        cK.append(c[:, 0::d, :, h, :])
    cKV = np.ascontiguousarray(np.stack(cK, axis=2))
    inA = {"xT": xT, "cm": cm, "wA": np.ascontiguousarray(w_in[:, colsA(h)]), "cKV": cKV}
    qk = qkcols(h)
    cst = inp["state_mlstm_conv"][0][:, :, qk]
    cst = np.ascontiguousarray(cst.reshape(32, 3, 4, 128).transpose(3, 2, 1, 0))
    wc = inp["w_conv"][0][:, qk].reshape(4, 4, 128)
    bc = inp["b_conv"][0][qk].reshape(4, 128)
    cpar = np.ascontiguousarray(np.concatenate([wc.transpose(2, 1, 0), bc.T[:, :, None]], axis=2)).astype(np.float32)
    gpar = np.ascontiguousarray(np.broadcast_to(np.array([inp["b_igate"][0, h], inp["b_fgate"][0, h]], np.float32), (128, 2)))
    gnorm = np.ascontiguousarray(np.broadcast_to(inp["g_mlstm_norm"][0, h * 512:(h + 1) * 512], (128, 512)))
    n0 = np.ascontiguousarray(inp["state_mlstm_n"][0][:, h].reshape(32, 2, 128).transpose(2, 1, 0))
    m0 = np.ascontiguousarray(np.broadcast_to(inp["state_mlstm_m"][0][:, h], (128, 32)))
    inB = {"xT": xT, "cm": cm, "wB": np.ascontiguousarray(w_in[:, colsB(h)]), "cst": cst,
           "c0": np.ascontiguousarray(inp["state_mlstm_c"][0][:, h]), "n0": n0, "m0": m0, "cpar": cpar, "gpar": gpar, "gnorm": gnorm}
    return inA, inB


import ml_dtypes
from concourse.bass_utils import run_bass_kernel_spmd

_PROGS = {}


def _onehot_ident(n, j):
    m = np.zeros((128, n, 128), np.float32)
    m[:, j, :] = np.eye(128, dtype=np.float32)
    return m.astype(ml_dtypes.bfloat16)


def kernel(**inp):
    inp = {k_: np.asarray(v) for k_, v in inp.items()}
    B, T_, NS = 2, inp["x_prompt"].shape[1], 32
    NPq = T_ // 4
    xs = inp["x_sample"][:, 0]
    cores = [(b, h) for b in range(B) for h in range(4)]
    lns = [inp[n][0] for n in ("ln1_g", "ln1_b", "ln2_g", "ln2_b", "ln3_g", "ln3_b")]
    lnp = np.ascontiguousarray(np.stack([l.reshape(16, 128).T for l in lns], axis=1)).astype(np.float32)
    wg = np.ascontiguousarray(inp["w_in"][0][:, 10760:14856])
    in_maps = []
    for c, (b, q) in enumerate(cores):
        inA, inB = prep_l1(q, inp["x_prompt"][b], xs, inp)
        m = dict(inA)
        m.update(inB)
        tsl = slice(NPq * q, NPq * (q + 1))
        m.update({
            "x2T": np.ascontiguousarray(np.concatenate([inp["x_prompt"][b, tsl], xs[4 * c:4 * c + 4]], 0).T),
            "selq": _onehot_ident(4, q), "sels": _onehot_ident(8, c), "wg": wg,
            "w_br_a": inp["w_branch_att"][0], "w_br_m": inp["w_branch_mlstm"][0], "w_mix": inp["w_mix_out"][0],
            "w_cq": inp["w_cross_q"][0], "w_mk": inp["w_mem_k"][0], "w_mv": inp["w_mem_v"][0],
            "w_co": inp["w_cross_out"][0], "w_up": inp["w_up"][0], "w_down": inp["w_down"][0],
            "memT": np.ascontiguousarray(inp["mem_prompt"][b].T),
            "skv": np.ascontiguousarray(inp["cache_mem_kv"][0][4 * c:4 * c + 4].reshape(4, 256, 2, 512)),
            "lnp": lnp, "ident": np.eye(128, dtype=np.float32)})
        in_maps.append(m)
    if T_ not in _PROGS:
        _PROGS[T_] = build_fused(T_=T_)
    res = run_bass_kernel_spmd(_PROGS[T_], in_maps, core_ids=list(range(8))).results
    res = [{k_: np.asarray(v) for k_, v in r.items()} for r in res]
    f = np.float32
    ST1 = 2048
    y_p = np.zeros((B, T_, 2048), f)
    y_s = np.zeros((NS, 1, 2048), f)
    keeps = tuple(min(w, T_) for w in (128, 512, 2048))
    p_kv = [np.zeros((1, B, kp, 2, 4, 128), f) for kp in keeps]
    p_mem = np.zeros((1, B, 256, 2, 4, 128), f)
    p_conv = np.zeros((1, B, 3, 2048), f)
    p_c = np.zeros((1, B, 4, 256, 512), f)
    p_n = np.zeros((1, B, 4, 256), f)
    p_m = np.zeros((1, B, 4), f)
    s_kv = [np.zeros((1, NS, 1, 2, 4, 128), f) for _ in range(3)]
    s_conv = np.zeros((1, NS, 3, 2048), f)
    s_c = np.zeros((1, NS, 4, 256, 512), f)
    s_n = np.zeros((1, NS, 4, 256), f)
    s_m = np.zeros((1, NS, 4), f)
    for c, (b, h) in enumerate(cores):
        r = res[c]
        yT = r["yT"]
        y_p[b, NPq * h:NPq * (h + 1)] = yT[:, :NPq].T
        y_s[4 * c:4 * c + 4, 0] = yT[:, NPq:NPq + 4].T
        if h == 0:
            p_mem[0, b] = r["pmem"].reshape(256, 2, 4, 128)
        kvT = r["kvT_o"]
        for g, kp in enumerate(keeps):
            for kv in range(2):
                p_kv[g][0, b, :, kv, h, :] = kvT[g, kv, :, ST1 - kp:ST1].T
                if b == 0:
                    s_kv[g][0, :, 0, kv, h, :] = kvT[g, kv, :, ST1:ST1 + NS].T
        qk = qkcols(h)
        p_conv[0, b][:, qk] = r["conv_o"].transpose(2, 1, 0).reshape(3, 512)
        p_c[0, b, h] = r["pc_o"]
        p_n[0, b, h] = r["pn_o"].T.reshape(256)
        p_m[0, b, h] = r["pm_o"][0, 0]
        if b == 0:
            s_conv[0][:, :, qk] = r["sconv_o"].transpose(3, 2, 1, 0).reshape(NS, 3, 512)
            s_c[0, :, h] = r["sc_o"]
            s_n[0, :, h] = r["sn_o"].transpose(2, 1, 0).reshape(NS, 256)
            s_m[0, :, h] = r["sm_o"][0]
    return (y_p, y_s, p_kv[0], p_kv[1], p_kv[2], p_mem, p_conv, p_c, p_n, p_m,
            s_kv[0], s_kv[1], s_kv[2], s_conv, s_c, s_n, s_m)
```

```python
import numpy as np
from contextlib import ExitStack
import concourse.bass as bass
import concourse.mybir as mybir

F32 = mybir.dt.float32
BF16 = mybir.dt.bfloat16
I32 = mybir.dt.int32
AF = mybir.ActivationFunctionType
ALU = mybir.AluOpType
AX = mybir.AxisListType

ENGS = ("pe", "act", "dve", "pool", "sp")


class Buf:
    __slots__ = ("name", "w", "r", "sem", "dcount", "excl")

    registry = None
    birth = None

    def __init__(self, name="", excl=False):
        self.name = name
        self.excl = excl
        self.w = Buf.birth
        if Buf.registry is not None:
            Buf.registry.append(self)
        self.r = []
        self.sem = None
        self.dcount = 0


class T:
    __slots__ = ("bufs", "ap")

    def __init__(self, buf, ap):
        self.bufs = list(buf) if isinstance(buf, (list, tuple)) else [buf]
        self.ap = ap

    @property
    def buf(self):
        return self.bufs[0]

    def __getitem__(self, idx):
        return T(self.bufs, self.ap[idx])

    def v(self, ap):
        return T(self.bufs, ap)


def _bufs(items):
    out = []
    for t in items:
        if isinstance(t, T):
            out.extend(t.bufs)
        else:
            out.append(t)
    return out


class KB:
    def __init__(self, nc, n_dma_sems=120):
        self.nc = nc
        self.es = ExitStack()
        self.ops = {e: [] for e in ENGS}
        self.flag = {e: [] for e in ENGS}
        self.waited = {e: {} for e in ENGS}
        self.n_dma_sems = n_dma_sems
        self.dma_sem_used = 0
        self.esem = {}
        self.dsem = []
        self.uid = 0
        self.scopes = []
        self.n_cc = 0
        Buf.registry = None
        Buf.birth = None
        self._scr = self.tile([128, 8], F32, "kb_scratch")

    def push_scope(self):
        self.scopes.append((self.es, Buf.registry))
        self.es = ExitStack()
        Buf.registry = []

    def pop_scope(self):
        bufs = list(Buf.registry)
        scr = self._scr
        self.op("dve", lambda h: h.memset(scr.ap[:, 0:1], 0.0), reads=bufs, writes=bufs + [scr])
        tok = ("e", "dve", len(self.ops["dve"]) - 1)
        self.es.close()
        self.es, Buf.registry = self.scopes.pop()
        Buf.birth = tok

    def collective(self, kind, in_h, out_h, groups, reads=(), writes=()):
        assert self.dma_sem_used < self.n_dma_sems
        sem = self.dma_sem_used
        self.dma_sem_used += 1
        waits = self._deps("pool", _bufs(reads), _bufs(writes))
        tok = ("d", sem, 1)
        self.ops["pool"].append(("cc", (kind, in_h, out_h, groups), waits, sem, {}))
        self.flag["pool"].append(False)
        self._commit(tok, _bufs(reads), _bufs(writes))

    def sb(self, shape, dtype, name=None):
        self.uid += 1
        return self.es.enter_context(self.nc.sbuf_tensor(name or f"sb{self.uid}", list(shape), dtype))

    def ps(self, shape, dtype, name=None):
        self.uid += 1
        return self.es.enter_context(self.nc.psum_tensor(name or f"ps{self.uid}", list(shape), dtype))

    def tile(self, shape, dtype, name=None):
        t = self.sb(shape, dtype, name)
        return T(Buf(name or ""), t[tuple(slice(None) for _ in shape)])

    def ptile(self, shape, dtype, name=None):
        t = self.ps(shape, dtype, name)
        return T(Buf(name or "", excl=True), t[tuple(slice(None) for _ in shape)])

    def _deps(self, eng, reads, writes):
        toks = []
        for b in _bufs(reads):
            if b.w is not None:
                toks.append(b.w)
        for b in _bufs(writes):
            if b.w is not None:
                toks.append(b.w)
            toks.extend(b.r)
        waits = []
        wd = self.waited[eng]
        for tok in toks:
            kind, src, val = tok
            if kind == "e" and src == eng and eng == "pe":
                continue
            key = (kind, src)
            if wd.get(key, -1) >= val:
                continue
            wd[key] = val
            if kind == "e":
                self.flag[src][val] = True
            waits.append(tok)
        best = {}
        for tok in waits:
            key = (tok[0], tok[1])
            if key not in best or best[key][2] < tok[2]:
                best[key] = tok
        return list(best.values())

    def _commit(self, tok, reads, writes):
        for b in _bufs(reads):
            b.r.append(tok)
        for b in _bufs(writes):
            b.w = tok
            b.r = []

    def op(self, eng, fn, reads=(), writes=()):
        rb, wb = _bufs(reads), _bufs(writes)
        writes = wb + [b for b in rb if b.excl and b not in wb]
        reads = [b for b in rb if not b.excl]
        waits = self._deps(eng, reads, writes)
        idx = len(self.ops[eng])
        self.ops[eng].append(("c", fn, waits))
        self.flag[eng].append(False)
        self._commit(("e", eng, idx), reads, writes)

    def dma(self, queue, out_ap, in_ap, reads=(), writes=(), **kw):
        pairs = out_ap if isinstance(out_ap, list) else [(out_ap, in_ap)]
        tb = _bufs(list(writes) + list(reads))[0]
        if tb.sem is None:
            tb.sem = self.dma_sem_used
            self.dma_sem_used += 1
            assert self.dma_sem_used <= self.n_dma_sems, "out of dma sems"
        waits = self._deps(queue, reads, writes)
        tb.dcount += 16 * len(pairs)
        tok = ("d", tb.sem, tb.dcount)
        idx = len(self.ops[queue])
        self.ops[queue].append(("d", pairs, waits, tb.sem, kw))
        self.flag[queue].append(False)
        self._commit(tok, reads, writes)
        return tok

    def wait_all(self, eng, bufs):
        waits = self._deps(eng, [], _bufs(bufs))
        self.ops[eng].append(("c", None, waits))
        self.flag[eng].append(False)

    def emit(self):
        nc = self.nc
        with ExitStack() as es:
            for e in ENGS:
                self.esem[e] = es.enter_context(nc.semaphore(f"prog_{e}"))
            for i in range(self.dma_sem_used):
                self.dsem.append(es.enter_context(nc.semaphore(f"dma_{i}")))
            evno = {}
            for e in ENGS:
                c = 0
                arr = []
                for f in self.flag[e]:
                    if f:
                        c += 1
                    arr.append(c)
                evno[e] = arr
            block = es.enter_context(nc.Block())

            def replay(e, h):
                for i, rec in enumerate(self.ops[e]):
                    waits = rec[2]
                    for kind, src, val in waits:
                        if kind == "e":
                            assert self.flag[src][val]
                            h.wait_ge(self.esem[src], evno[src][val])
                        else:
                            h.wait_ge(self.dsem[src], val)
                    if rec[0] == "c":
                        if rec[1] is None:
                            continue
                        inst = rec[1](h)
                        if self.flag[e][i]:
                            inst.then_inc(self.esem[e], 1)
                    elif rec[0] == "cc":
                        kind, in_h, out_h, groups = rec[1]
                        h.collective_compute(kind, mybir.AluOpType.bypass, replica_groups=groups,
                                             ins=[in_h.ap().opt()], outs=[out_h.ap().opt()]).then_inc(self.dsem[rec[3]])
                    else:
                        for (o, a) in rec[1]:
                            h.dma_start(out=o, in_=a, **rec[4]).then_inc(self.dsem[rec[3]], 16)

            @block.tensor
            def _(h):
                replay("pe", h)

            @block.scalar
            def _(h):
                replay("act", h)

            @block.vector
            def _(h):
                replay("dve", h)

            @block.gpsimd
            def _(h):
                replay("pool", h)

            @block.sync
            def _(h):
                replay("sp", h)
        self.es.close()

    def mm(self, out, lhsT, rhs, start=True, stop=True):
        self.op("pe", lambda h: h.matmul(out.ap, lhsT.ap, rhs.ap, start=start, stop=stop),
                reads=[lhsT, rhs] + ([] if start else [out]), writes=[out])

    def transpose(self, out, in_, ident):
        self.op("pe", lambda h: h.transpose(out.ap, in_.ap, ident.ap), reads=[in_, ident], writes=[out])

    def act(self, out, in_, func, bias=None, scale=1.0, eng="act", extra_reads=()):
        rd = [in_] + list(extra_reads)
        kw = {}
        if bias is not None:
            if isinstance(bias, T):
                rd.append(bias)
                kw["bias"] = bias.ap
            else:
                kw["bias"] = bias
        if isinstance(scale, T):
            rd.append(scale)
            kw["scale"] = scale.ap
        else:
            kw["scale"] = scale
        self.op("act", lambda h: h.activation(out.ap, in_.ap, func, **kw), reads=rd, writes=[out])

    def tt(self, eng, out, in0, in1, op):
        self.op(eng, lambda h: h.tensor_tensor(out.ap, in0.ap, in1.ap, op), reads=[in0, in1], writes=[out])

    def ts(self, eng, out, in0, s1, s2, op0, op1=None):
        rd = [in0]
        a1 = s1.ap if isinstance(s1, T) else s1
        a2 = s2.ap if isinstance(s2, T) else s2
        if isinstance(s1, T):
            rd.append(s1)
        if isinstance(s2, T):
            rd.append(s2)
        if op1 is None:
            self.op(eng, lambda h: h.tensor_scalar(out.ap, in0.ap, a1, None, op0), reads=rd, writes=[out])
        else:
            self.op(eng, lambda h: h.tensor_scalar(out.ap, in0.ap, a1, a2, op0, op1), reads=rd, writes=[out])

    def stt(self, eng, out, in0, scalar, in1, op0, op1):
        rd = [in0, in1]
        a = scalar.ap if isinstance(scalar, T) else scalar
        if isinstance(scalar, T):
            rd.append(scalar)
        self.op(eng, lambda h: h.scalar_tensor_tensor(out.ap, in0.ap, a, in1.ap, op0, op1), reads=rd, writes=[out])

    def copy(self, eng, out, in_):
        if eng == "act":
            self.op(eng, lambda h: h.copy(out.ap, in_.ap), reads=[in_], writes=[out])
        else:
            self.op(eng, lambda h: h.tensor_copy(out.ap, in_.ap), reads=[in_], writes=[out])

    def memset(self, eng, out, val):
        self.op(eng, lambda h: h.memset(out.ap, val), reads=[], writes=[out])

    def reduce(self, eng, out, in_, op, axis=AX.X):
        self.op(eng, lambda h: h.tensor_reduce(out.ap, in_.ap, axis, op), reads=[in_], writes=[out])


D = 2048
ALPHA = 2.0 ** 0.25
LN_EPS = 1e-5
KC = 16


def l2_io(nc, NP, NS, DFF, io=None, fused=False):
    io = {} if io is None else io
    NT = NP + NS

    def din(name, shape, dt=F32):
        io[name] = nc.dram_tensor(name, list(shape), dt, kind="ExternalInput").ap()

    def dout(name, shape, dt=F32):
        io[name] = nc.dram_tensor(name, list(shape), dt, kind="ExternalOutput").ap()

    din("x2T", [D, NT])
    if fused:
        din("selq", [128, 4, 128], BF16)
        din("sels", [128, 8, 128], BF16)
    else:
        din("attT", [512, NT], BF16)
        din("hmT", [D, NT], BF16)
    din("wg", [D, 2 * D])
    din("w_br_a", [512, D])
    din("w_br_m", [D, D])
    din("w_mix", [D, D])
    din("w_cq", [D, 512])
    din("w_mk", [D, 512])
    din("w_mv", [D, 512])
    din("w_co", [512, D])
    din("w_up", [D, DFF])
    din("w_down", [DFF, D])
    din("memT", [D, 256])
    din("skv", [max(NS, 1), 256, 2, 512])
    din("lnp", [128, 6, KC])
    din("ident", [128, 128])
    dout("yT", [D, NT])
    dout("pmem", [256, 2, 512])
    return io


def build_l2(NP=2048, NS=4, DFF=8192, ST=1024):
    nc = bass.Bass("TRN2", target_bir_lowering=False)
    io = l2_io(nc, NP, NS, DFF)
    k = KB(nc)
    sh = {"pss": [k.ptile([128, 512], F32, f"ps{i}") for i in range(6)], "pi": [0]}
    emit_l2(k, io, sh, NP, NS, DFF, ST)
    k.emit()
    return nc


def emit_l2(k, io, sh, NP=2048, NS=4, DFF=8192, ST=1024, G=None):
    NT = NP + NS
    FC = DFF // 128
    xT = io["x2T"]
    if G is None:
        attT, hmT = io["attT"], io["hmT"]
    wg, w_br_a, w_br_m, w_mix, w_cq, w_mk, w_mv, w_co, w_up, w_down = (io[n_] for n_ in (
        "wg", "w_br_a", "w_br_m", "w_mix", "w_cq", "w_mk", "w_mv", "w_co", "w_up", "w_down"))
    memT, skv, lnp, ident_d, yT, pmem = (io[n_] for n_ in ("memT", "skv", "lnp", "ident", "yT", "pmem"))

    TTM = ST + NS
    assert TTM % 2 == 0
    XH = k.sb([128, 2 * KC * TTM], BF16, "XH")
    mg_t = k.sb([128, KC, TTM], BF16, "mg")
    G_t = k.sb([128, 8, TTM], BF16, "G")
    xin = [T(Buf(f"xin{i}"), XH[:, i * TTM:(i + 1) * TTM]) for i in range(KC)]
    hmi = [T(Buf(f"hmi{i}"), XH[:, (KC + i) * TTM:(KC + i + 1) * TTM]) for i in range(KC)]
    _al = xin + hmi
    resid = [T([Buf(f"resid{i}"), _al[2 * i].buf, _al[2 * i + 1].buf],
               XH[:, 2 * i * TTM:(2 * i + 2) * TTM].bitcast(F32)) for i in range(KC)]
    mg = [T(Buf(f"mg{i}"), mg_t[:, i, :]) for i in range(KC)]
    xbf = mg
    hg = [T(Buf(f"G{i}"), G_t[:, i, :]) for i in range(8)]
    atc = hg[0:4]
    occ = hg[4:8]
    NW = 5
    wslots = [k.tile([128, KC, 128], BF16, f"w{i}") for i in range(NW)]
    wi = [0]
    pss, pi = sh["pss"], sh["pi"]
    xst = [k.tile([128, 512], F32, f"xst{i}") for i in range(4)]
    xsi = [0]
    tmpf = [k.tile([128, 512], F32, f"tmpf{i}") for i in range(4)]
    tfi = [0]
    tmpb = [k.tile([128, 512], BF16, f"tmpb{i}") for i in range(4)]
    tbi = [0]
    consts = Buf("consts")
    lnp_t = T(consts, k.sb([128, 6, KC], F32, "lnp_sb")[:, :, :])
    ident_f = T(consts, k.sb([128, 128], F32, "ident_sb")[:, :])
    ident_b = k.tile([128, 128], BF16, "identb2")
    ones_b = k.tile([128, 128], BF16, "onesb2")
    mean_bc = k.tile([128, 512], F32, "mean")
    rstd_bc = k.tile([128, 512], F32, "rstd")
    memT_b = k.tile([128, KC, 256], BF16, "memTb")
    kT_b = k.tile([128, 4, 256], BF16, "kTb")
    v_b = k.tile([128, 2, 512], BF16, "vb")
    skv_f = k.tile([128, 2, 512], F32, "skvf")
    skT_b = k.tile([128, 4, 256], BF16, "skTb")
    sv_b = k.tile([128, 2, 512], BF16, "svb")
    yst = xst
    ysi = xsi

    def nxt(lst, ctr):
        t = lst[ctr[0] % len(lst)]
        ctr[0] += 1
        return t

    if G is not None:
        selq_t = k.tile([128, 4, 128], BF16, "selq_t")
        k.dma("sp", selq_t.ap, io["selq"], writes=[selq_t])
        sels_t = k.tile([128, 8, 128], BF16, "sels_t")
        k.dma("sp", sels_t.ap, io["sels"], writes=[sels_t])
        cands = [k.tile([128, 4, 512], BF16, f"cand{i}") for i in range(2)]
        cdi = [0]
        cands8 = [k.tile([128, 8, 4], BF16, f"cand8_{i}") for i in range(2)]
        cd8i = [0]
    k.dma("sp", lnp_t.ap, lnp, writes=[lnp_t])
    k.dma("sp", ident_f.ap, ident_d, writes=[ident_f])
    k.copy("dve", ident_b, ident_f)
    k.memset("dve", ones_b, 1.0)

    def wview(W, r0, nk, c0, ncol=128):
        return W[r0:r0 + nk * 128, c0:c0 + ncol].rearrange("(kc p) n -> p kc n", p=128)

    def load_w(W, r0, nk, c0):
        ws = nxt(wslots, wi)
        k.dma("pool", ws.ap[:, 0:nk, :], wview(W, r0, nk, c0), writes=[ws])
        return ws

    def group(ws, nk, ins, off, ln):
        ps = nxt(pss, pi)
        for kc in range(nk):
            k.mm(ps[:, 0:ln], ws[:, kc, :], ins[kc][:, off:off + ln], start=(kc == 0), stop=(kc == nk - 1))
        return ps

    k.dma("pool", memT_b.ap, memT.rearrange("(kc p) n -> p kc n", p=128), writes=[memT_b])
    for kv, W in enumerate((w_mk, w_mv)):
        for hh in range(4):
            ws = load_w(W, 0, KC, hh * 128)
            if kv == 0:
                ps = nxt(pss, pi)
                for kc in range(KC):
                    k.mm(ps[:, 0:256], ws[:, kc, :], memT_b[:, kc, :], start=(kc == 0), stop=(kc == KC - 1))
                k.copy("dve", kT_b[:, hh, :], ps[:, 0:256])
            for mc in range(2):
                ps = nxt(pss, pi)
                for kc in range(KC):
                    k.mm(ps[:, 0:128], memT_b[:, kc, mc * 128:(mc + 1) * 128], ws[:, kc, :], start=(kc == 0), stop=(kc == KC - 1))
                ko = nxt(xst, xsi)
                k.copy("dve", ko[:, 0:128], ps[:, 0:128])
                if kv == 1:
                    k.copy("act", v_b[:, mc, hh * 128:(hh + 1) * 128], ps[:, 0:128])
                k.dma("sp", pmem[mc * 128:(mc + 1) * 128, kv, hh * 128:(hh + 1) * 128], ko.ap[:, 0:128], reads=[ko])

    def cross_attn(qc, kT, vv, outc, off, ln):
        scale = 128.0 ** -0.5
        for hh in range(4):
            pT = []
            for mc in range(2):
                ps = nxt(pss, pi)
                k.mm(ps[:, 0:ln], kT[:, hh, mc * 128:(mc + 1) * 128], qc[hh][:, off:off + ln])
                pb = nxt(tmpb, tbi)
                k.act(pb[:, 0:ln], ps[:, 0:ln], AF.Exp, scale=scale)
                pT.append(pb)
            psd = nxt(pss, pi)
            for mc in range(2):
                k.mm(psd[:, 0:ln], ones_b, pT[mc][:, 0:ln], start=(mc == 0), stop=(mc == 1))
            rden = nxt(tmpf, tfi)
            k.op("dve", lambda h, o=rden[:, 0:ln], i=psd[:, 0:ln]: h.reciprocal(o.ap, i.ap), reads=[psd], writes=[rden])
            pso = nxt(pss, pi)
            for mc in range(2):
                k.mm(pso[:, 0:ln], vv[:, mc, hh * 128:(hh + 1) * 128], pT[mc][:, 0:ln], start=(mc == 0), stop=(mc == 1))
            k.tt("dve", outc[hh][:, off:off + ln], pso[:, 0:ln], rden[:, 0:ln], ALU.mult)

    def layernorm(li, tiles, ln_out_bf, final_out=None, tok0=0):
        gcol = lambda kc: lnp_t[:, 2 * li, kc:kc + 1]
        bcol = lambda kc: lnp_t[:, 2 * li + 1, kc:kc + 1]
        for (off, ln) in tiles:
            ps_s = nxt(pss, pi)
            ps_q = nxt(pss, pi)
            for kc in range(KC):
                tb = nxt(tmpb, tbi)
                k.copy("act", tb[:, 0:ln], resid[kc][:, off:off + ln])
                tq = nxt(tmpb, tbi)
                k.act(tq[:, 0:ln], resid[kc][:, off:off + ln], AF.Square)
                k.mm(ps_s[:, 0:ln], ones_b, tb[:, 0:ln], start=(kc == 0), stop=(kc == KC - 1))
                k.mm(ps_q[:, 0:ln], ones_b, tq[:, 0:ln], start=(kc == 0), stop=(kc == KC - 1))
            k.ts("dve", mean_bc[:, 0:ln], ps_s[:, 0:ln], 1.0 / D, None, ALU.mult)
            msq = nxt(tmpf, tfi)
            k.tt("dve", msq[:, 0:ln], mean_bc[:, 0:ln], mean_bc[:, 0:ln], ALU.mult)
            var = nxt(tmpf, tfi)
            k.stt("dve", var[:, 0:ln], ps_q[:, 0:ln], 1.0 / D, msq[:, 0:ln], ALU.mult, ALU.subtract)
            k.ts("dve", var[:, 0:ln], var[:, 0:ln], LN_EPS, None, ALU.add)
            k.act(var[:, 0:ln], var[:, 0:ln], AF.Sqrt)
            k.op("dve", lambda h, o=rstd_bc[:, 0:ln], i=var[:, 0:ln]: h.reciprocal(o.ap, i.ap), reads=[var], writes=[rstd_bc])
            for kc in range(KC):
                t1 = nxt(tmpf, tfi)
                k.tt("dve", t1[:, 0:ln], resid[kc][:, off:off + ln], mean_bc[:, 0:ln], ALU.subtract)
                k.tt("dve", t1[:, 0:ln], t1[:, 0:ln], rstd_bc[:, 0:ln], ALU.mult)
                if final_out is None:
                    k.act(resid[kc][:, off:off + ln], t1[:, 0:ln], AF.Identity, bias=bcol(kc), scale=gcol(kc))
                    k.act(ln_out_bf[kc][:, off:off + ln], t1[:, 0:ln], AF.Identity, bias=bcol(kc), scale=gcol(kc))
                else:
                    ys = nxt(yst, ysi)
                    k.act(ys[:, 0:ln], t1[:, 0:ln], AF.Identity, bias=bcol(kc), scale=gcol(kc))
                    k.dma("sp", final_out[kc * 128:(kc + 1) * 128, tok0 + off:tok0 + off + ln], ys.ap[:, 0:ln], reads=[ys])

    sts = []
    t0 = 0
    while t0 < NP:
        n = min(ST, NP - t0)
        sts.append([t0, n, 0])
        t0 += n
    sts[-1][2] = NS
    for (t0, n, ns) in sts:
        TTn = n + ns
        tiles = [(o, min(512, n - o)) for o in range(0, n, 512)]
        if ns:
            tiles.append((n, ns))
        def dram_cols(ap2d, r0, nr):
            prs = [(0, ap2d[r0:r0 + nr, t0:t0 + n])]
            if ns:
                prs.append((n, ap2d[r0:r0 + nr, NP:NP + ns]))
            return prs
        for kc in range(KC):
            k.dma("pool", [(xin[kc].ap[:, o:o + a.shape[1]], a) for (o, a) in dram_cols(xT, kc * 128, 128)], None, writes=[xin[kc]])
        if G is None:
            for kc in range(KC):
                k.dma("sp", [(hmi[kc].ap[:, o:o + a.shape[1]], a) for (o, a) in dram_cols(hmT, kc * 128, 128)], None, writes=[hmi[kc]])
            for c in range(4):
                k.dma("sp", [(atc[c].ap[:, o:o + a.shape[1]], a) for (o, a) in dram_cols(attT, c * 128, 128)], None, writes=[atc[c]])
        else:
            for r in range(4):
                for kk in range(5):
                    dstt = atc[r] if kk == 0 else hmi[r * 4 + kk - 1]
                    for (off, ln) in tiles:
                        ps = nxt(pss, pi)
                        if off < n:
                            cd = nxt(cands, cdi)
                            prs, gb = [], []
                            for blk in range(4):
                                gt, gbuf = G["g"][(kk, blk // 2)]
                                c0_ = (blk % 2) * NP + t0 + off
                                prs.append((cd.ap[:, blk, 0:ln], gt[r * 128:(r + 1) * 128, c0_:c0_ + ln]))
                                gb.append(gbuf)
                            k.dma("sp", prs, None, reads=gb, writes=[cd])
                            for blk in range(4):
                                k.mm(ps[:, 0:ln], selq_t[:, blk, :], cd[:, blk, 0:ln], start=(blk == 0), stop=(blk == 3))
                        else:
                            cd = nxt(cands8, cd8i)
                            gt, gbuf = G["gs"]
                            k.dma("sp", cd.ap[:, :, 0:ln], gt[r * 128:(r + 1) * 128, kk * 32:(kk + 1) * 32].rearrange("p (j s) -> p j s", s=ln),
                                  reads=[gbuf], writes=[cd])
                            for j in range(8):
                                k.mm(ps[:, 0:ln], sels_t[:, j, :], cd[:, j, 0:ln], start=(j == 0), stop=(j == 7))
                        k.copy("act", dstt[:, off:off + ln], ps[:, 0:ln])
        for oc in range(KC):
            w_ga = load_w(wg, 0, KC, oc * 128)
            w_gm = load_w(wg, 0, KC, D + oc * 128)
            w_a = load_w(w_br_a, 0, 4, oc * 128)
            w_m = load_w(w_br_m, 0, KC, oc * 128)
            for (off, ln) in tiles:
                p_ga = group(w_ga, KC, xin, off, ln)
                p_a = group(w_a, 4, atc, off, ln)
                s1 = nxt(tmpf, tfi)
                k.act(s1[:, 0:ln], p_ga[:, 0:ln], AF.Sigmoid)
                k.tt("dve", s1[:, 0:ln], s1[:, 0:ln], p_a[:, 0:ln], ALU.mult)
                p_gm = group(w_gm, KC, xin, off, ln)
                p_m = group(w_m, KC, hmi, off, ln)
                s2 = nxt(tmpf, tfi)
                k.act(s2[:, 0:ln], p_gm[:, 0:ln], AF.Sigmoid)
                k.tt("dve", s2[:, 0:ln], s2[:, 0:ln], p_m[:, 0:ln], ALU.mult)
                k.tt("dve", mg[oc][:, off:off + ln], s1[:, 0:ln], s2[:, 0:ln], ALU.add)
        for oc in range(KC):
            ws = load_w(w_mix, 0, KC, oc * 128)
            for (off, ln) in tiles:
                xs = nxt(xst, xsi)
                src = xT[oc * 128:(oc + 1) * 128, t0 + off:t0 + off + ln] if off < n else xT[oc * 128:(oc + 1) * 128, NP:NP + ns]
                k.dma("sp", xs.ap[:, 0:ln], src, writes=[xs])
                ps = group(ws, KC, mg, off, ln)
                k.stt("dve", resid[oc][:, off:off + ln], xs[:, 0:ln], ALPHA, ps[:, 0:ln], ALU.mult, ALU.add)
        layernorm(0, tiles, xbf)
        for hc in range(4):
            ws = load_w(w_cq, 0, KC, hc * 128)
            for (off, ln) in tiles:
                ps = group(ws, KC, xbf, off, ln)
                k.copy("act", atc[hc][:, off:off + ln], ps[:, 0:ln])
        for (off, ln) in tiles:
            if off < n:
                cross_attn(atc, kT_b, v_b, occ, off, ln)
            else:
                for s in range(ns):
                    for mc in range(2):
                        k.dma("sp", skv_f.ap, skv[s, mc * 128:(mc + 1) * 128, :, :], writes=[skv_f])
                        k.copy("act", sv_b[:, mc, :], skv_f[:, 1, :])
                        for hh in range(4):
                            psx = nxt(pss, pi)
                            k.transpose(psx[:, 0:128], skv_f[:, 0, hh * 128:(hh + 1) * 128], ident_f)
                            k.copy("dve", skT_b[:, hh, mc * 128:(mc + 1) * 128], psx[:, 0:128])
                    cross_attn(atc, skT_b, sv_b, occ, off + s, 1)
        for oc in range(KC):
            ws = load_w(w_co, 0, 4, oc * 128)
            for (off, ln) in tiles:
                ps = group(ws, 4, occ, off, ln)
                k.stt("dve", resid[oc][:, off:off + ln], resid[oc][:, off:off + ln], ALPHA, ps[:, 0:ln], ALU.mult, ALU.add)
        layernorm(1, tiles, xbf)
        for oc in range(KC):
            for (off, ln) in tiles:
                k.ts("dve", resid[oc][:, off:off + ln], resid[oc][:, off:off + ln], ALPHA, None, ALU.mult)
        for g in range(FC // 8):
            for j in range(8):
                ws = load_w(w_up, 0, KC, (g * 8 + j) * 128)
                for (off, ln) in tiles:
                    ps = group(ws, KC, xbf, off, ln)
                    r1 = nxt(tmpf, tfi)
                    k.act(r1[:, 0:ln], ps[:, 0:ln], AF.Relu)
                    k.tt("dve", hg[j][:, off:off + ln], r1[:, 0:ln], r1[:, 0:ln], ALU.mult)
            for oc in range(KC):
                ws = load_w(w_down, g * 8 * 128, 8, oc * 128)
                for (off, ln) in tiles:
                    ps = group(ws, 8, hg, off, ln)
                    k.tt("dve", resid[oc][:, off:off + ln], resid[oc][:, off:off + ln], ps[:, 0:ln], ALU.add)
        ptiles = [(o, l) for (o, l) in tiles if o < n]
        layernorm(2, ptiles, None, final_out=yT, tok0=t0)
        if ns:
            layernorm(2, [(n, ns)], None, final_out=yT, tok0=NP - n)
    k.wait_all("sp", xst)


D = 2048
KC = 16
GROUPS = ((128, 1), (512, 4), (2048, 16))
SCALE = 128.0 ** -0.5
LN_EPS = 1e-5
NEG = -1.0e30


def make_shared(k, cm):
    sh = {}
    sh["pss"] = [k.ptile([128, 512], F32, f"ps{i}") for i in range(6)]
    sh["pi"] = [0]
    sh["pst"] = [k.ptile([128, 1024], BF16, f"pst{i}") for i in range(2)]
    sh["pti"] = [0]
    cmt = k.tile([128, 6, 128], F32, "cmt")
    k.dma("sp", cmt.ap, cm, writes=[cmt])
    sh["cmt"] = cmt
    ident_b = k.tile([128, 128], BF16, "identb")
    k.copy("dve", ident_b, cmt[:, 0, :])
    mask_cur = k.tile([128, 128], BF16, "mcur")
    k.copy("dve", mask_cur, cmt[:, 1, :])
    mask_prev = k.tile([128, 128], BF16, "mprev")
    k.copy("dve", mask_prev, cmt[:, 2, :])
    ones_b = k.tile([128, 128], BF16, "onesb")
    k.memset("dve", ones_b, 1.0)
    ones_f = k.tile([128, 128], F32, "onesf")
    k.memset("dve", ones_f, 1.0)
    negI_b = k.tile([128, 128], BF16, "negIb")
    k.ts("dve", negI_b, cmt[:, 0, :], -1.0, None, ALU.mult)
    posm_b = k.tile([128, 128], BF16, "posmb")
    k.copy("dve", posm_b, cmt[:, 5, :])
    negm_b = k.tile([128, 128], BF16, "negmb")
    k.copy("dve", negm_b, cmt[:, 3, :])
    sh.update(ident_b=ident_b, mask_cur=mask_cur, mask_prev=mask_prev, ones_b=ones_b, ones_f=ones_f,
              negI_b=negI_b, posm_b=posm_b, negm_b=negm_b)
    return sh


def l1_io(nc, part, T_, NSMP, ST, io=None):
    io = {} if io is None else io
    NTT = T_ + NSMP

    def din(name, shape, dt=F32):
        if name not in io:
            io[name] = nc.dram_tensor(name, list(shape), dt, kind="ExternalInput").ap()

    def dout(name, shape, dt=F32):
        io[name] = nc.dram_tensor(name, list(shape), dt, kind="ExternalOutput").ap()

    din("xT", [D, NTT])
    din("cm", [128, 6, 128])
    if part == "A":
        din("wA", [D, 1152])
        din("cKV", [NSMP, 128, 3, 2, 128])
        dout("kvT_o", [3, 2, 128, ST + NSMP])
    else:
        din("wB", [D, 1538])
        din("cst", [128, 4, 3, NSMP])
        din("c0", [NSMP, 256, 512])
        din("n0", [128, 2, NSMP])
        din("m0", [128, NSMP])
        din("cpar", [128, 4, 5])
        din("gpar", [128, 2])
        din("gnorm", [128, 512])
        dout("conv_o", [128, 4, 3])
        dout("sconv_o", [128, 4, 3, NSMP])
        dout("pc_o", [256, 512])
        dout("pn_o", [128, 2])
        dout("pm_o", [128, 1])
        dout("sc_o", [NSMP, 256, 512])
        dout("sn_o", [128, 2, NSMP])
        dout("sm_o", [128, NSMP])
    return io


def build_l1(part, T_=8192, NSMP=32, ST=2048):
    nc = bass.Bass("TRN2", target_bir_lowering=False)
    io = l1_io(nc, part, T_, NSMP, ST)
    NTT = T_ + NSMP
    if part == "A":
        o = nc.dram_tensor("attT_o", [128, NTT], BF16, kind="ExternalOutput").ap()
        io["att_dst"] = lambda c0, n: [(o[:, c0:c0 + n], 0, n, [])]
    else:
        o = nc.dram_tensor("hmT_o", [512, NTT], BF16, kind="ExternalOutput").ap()
        io["hm_dst"] = lambda c, c0, n: [(o[c * 128:(c + 1) * 128, c0:c0 + n], 0, n, [])]
    k = KB(nc)
    sh = make_shared(k, io["cm"])
    emit_l1(k, part, io, sh, T_, NSMP, ST)
    k.emit()
    return nc


def emit_l1(k, part, io, sh, T_=8192, NSMP=32, ST=2048):
    NTT = T_ + NSMP
    NST = T_ // ST
    xT = io["xT"]
    if part == "A":
        wA, cKV, kvT_o, att_dst = io["wA"], io["cKV"], io["kvT_o"], io["att_dst"]
    else:
        wB, cst, c0, n0, m0, cpar, gpar, gnorm = (io[n_] for n_ in ("wB", "cst", "c0", "n0", "m0", "cpar", "gpar", "gnorm"))
        conv_o, sconv_o, pc_o, pn_o, pm_o, sc_o, sn_o, sm_o = (io[n_] for n_ in ("conv_o", "sconv_o", "pc_o", "pn_o", "pm_o", "sc_o", "sn_o", "sm_o"))
        hm_dst = io["hm_dst"]

    def nxt(lst, ctr):
        t = lst[ctr[0] % len(lst)]
        ctr[0] += 1
        return t

    pss, pi, pst, pti = sh["pss"], sh["pi"], sh["pst"], sh["pti"]
    cmt = sh["cmt"]
    ident_f = cmt[:, 0, :]
    tri_f = cmt[:, 1, :]
    negm_f = cmt[:, 3, :]
    posm_f = cmt[:, 5, :]
    padneg = cmt[:, 4, 0:1]
    padone = cmt[:, 4, 1:2]
    ident_b, mask_cur, mask_prev, ones_b, ones_f = (sh[n_] for n_ in ("ident_b", "mask_cur", "mask_prev", "ones_b", "ones_f"))
    xts = [k.tile([128, KC, 512], BF16, f"xt{part}{i}") for i in range(2)]
    xi = [0]
    if part == "A":
        ostf = [k.tile([128, 512], F32, f"ostf{part}{i}") for i in range(3)]
        ofi = [0]
        tmpb = [k.tile([128, 128], BF16, f"tb{part}{i}") for i in range(6)]
        tbi = [0]

    def load_x(t0, n):
        xt = nxt(xts, xi)
        k.dma("pool", [(xt.ap[:, kc, 0:n], xT[kc * 128:(kc + 1) * 128, t0:t0 + n]) for kc in range(KC)], None, writes=[xt])
        return xt

    tiles = [(t0, 512) for t0 in range(0, T_, 512)] + [(T_, NSMP)]
    xpre = {}
    n_pref = len(tiles) if part == "A" else len(tiles) - 1

    def get_x(idx):
        if idx not in xpre:
            xpre[idx] = load_x(*tiles[idx])
        xt = xpre.pop(idx)
        if idx + 1 < n_pref and (idx + 1) not in xpre:
            xpre[idx + 1] = load_x(*tiles[idx + 1])
        return xt

    if part == "A":
        wAb = k.tile([128, KC, 1152], BF16, "wAb")
        k.dma("pool", [(wAb.ap[:, kc, :], wA[kc * 128:(kc + 1) * 128, :]) for kc in range(KC)], None, writes=[wAb])
        W2 = 2 * ST
        qT = [k.tile([128, ST], BF16, f"qT{g}") for g in range(3)]
        kT = [k.tile([128, W2], BF16, f"kT{g}") for g in range(3)]
        vT = [k.tile([128, W2], BF16, f"vT{g}") for g in range(3)]
        numA = k.tile([128, ST], F32, "numA")
        denA = k.tile([128, ST], F32, "denA")
        atto = k.tile([128, ST], BF16, "atto")
        Vb = [k.tile([128, 128], BF16, f"Vb{i}") for i in range(2)]
        sq = [k.tile([128, NSMP], BF16, f"sq{g}") for g in range(3)]
        sk = [k.tile([128, NSMP], BF16, f"sk{g}") for g in range(3)]
        sv = [k.tile([128, NSMP], BF16, f"sv{g}") for g in range(3)]
        ckv = [k.tile([128, 3, 2, 128], F32, f"ckv{i}") for i in range(2)]
        cki = [0]
        cKT_b = k.tile([128, 128], BF16, "cKTb")
        cV_b = k.tile([128, 128], BF16, "cVb")
        numS = k.tile([128, NSMP], F32, "numS")
        denS = k.tile([128, NSMP], F32, "denS")

        def proj_A(xt, n, dst_fn, kv_out_col=None):
            for blk in range(9):
                ps = nxt(pss, pi)
                for kc in range(KC):
                    k.mm(ps[:, 0:n], wAb[:, kc, blk * 128:(blk + 1) * 128], xt[:, kc, 0:n], start=(kc == 0), stop=(kc == KC - 1))
                k.copy("act", dst_fn(blk), ps[:, 0:n])
                if kv_out_col is not None and blk >= 3:
                    of = nxt(ostf, ofi)
                    k.copy("dve", of[:, 0:n], ps[:, 0:n])
                    g, kv = (blk - 3) % 3, (blk - 3) // 3
                    k.dma("sp", kvT_o[g, kv, :, kv_out_col:kv_out_col + n], of.ap[:, 0:n], reads=[of])

        def att_unit(q_ap, kblocks, acc_num, acc_den, first):
            nq = q_ap.ap.shape[1]
            es = []
            for (kt, vb, mask) in kblocks:
                ps = nxt(pss, pi)
                if mask is None:
                    k.mm(ps[:, 0:nq], kt, q_ap)
                else:
                    k.mm(ps[:, 0:nq], kt, q_ap, start=True, stop=False)
                    if mask is mask_cur:
                        k.mm(ps[:, 0:nq], sh["negI_b"], sh["posm_b"][:, 0:nq], start=False, stop=True)
                    else:
                        k.mm(ps[:, 0:nq], ident_b, sh["negm_b"][:, 0:nq], start=False, stop=True)
                e = nxt(tmpb, tbi)
                k.act(e[:, 0:nq], ps[:, 0:nq], AF.Exp, scale=SCALE)
                es.append(e)
            psn = nxt(pss, pi)
            psd = nxt(pss, pi)
            nb = len(kblocks)
            for i, (kt, vb, mask) in enumerate(kblocks):
                k.mm(psn[:, 0:nq], vb, es[i][:, 0:nq], start=(i == 0), stop=(i == nb - 1))
            for i in range(nb):
                k.mm(psd[:, 0:nq], ones_b, es[i][:, 0:nq], start=(i == 0), stop=(i == nb - 1))
            if first:
                k.copy("dve", acc_num, psn[:, 0:nq])
                k.copy("act", acc_den, psd[:, 0:nq])
            else:
                k.tt("dve", acc_num, acc_num, psn[:, 0:nq], ALU.add)
                k.tt("dve", acc_den, acc_den, psd[:, 0:nq], ALU.add)

        def vtrans(src_ap, dst):
            pt = nxt(pst, pti)
            k.transpose(pt[:, 0:128], src_ap, ident_b)
            k.copy("dve", dst, pt[:, 0:128])

        for J in range(NST):
            cur = (J % 2) * ST
            prv = ((J + 1) % 2) * ST
            last = (J == NST - 1)
            for jj in range(ST // 512):
                t0 = J * ST + jj * 512
                xt = get_x(t0 // 512)
                o = jj * 512

                def dst(blk, o=o):
                    g, kind = blk % 3, blk // 3
                    if kind == 0:
                        return qT[g][:, o:o + 512]
                    return (kT if kind == 1 else vT)[g][:, cur + o:cur + o + 512]
                proj_A(xt, 512, dst, kv_out_col=(o if last else None))
            for g, (win, d) in enumerate(GROUPS):
                nb = ST // (128 * d)
                for r in range(d):
                    vprev = None
                    for n in range(nb):
                        sl = slice(r + d * 128 * n, r + d * 128 * n + d * 127 + 1, d)
                        slc = slice(cur + sl.start, cur + sl.stop, d)
                        blocks = []
                        if n == 0:
                            if J > 0:
                                slp = slice(prv + r + d * 128 * (nb - 1), prv + r + d * 128 * (nb - 1) + d * 127 + 1, d)
                                vprev = nxt(Vb, [n + 1])
                                vtrans(vT[g][:, slp], vprev)
                                blocks.append((kT[g][:, slp], vprev, mask_prev))
                        else:
                            slp = slice(cur + r + d * 128 * (n - 1), cur + r + d * 128 * (n - 1) + d * 127 + 1, d)
                            blocks.append((kT[g][:, slp], vprev, mask_prev))
                        vcur = Vb[n % 2]
                        vtrans(vT[g][:, slc], vcur)
                        blocks.append((kT[g][:, slc], vcur, mask_cur))
                        att_unit(qT[g][:, sl], blocks, numA[:, sl], denA[:, sl], first=(g == 0))
                        vprev = vcur
            k.op("dve", lambda h: h.reciprocal(denA.ap, denA.ap), reads=[denA], writes=[denA])
            k.tt("dve", atto, numA, denA, ALU.mult)
            for (dst, so, nn, dbufs) in att_dst(J * ST, ST):
                k.dma("sp", dst, atto.ap[:, so:so + nn], reads=[atto], writes=dbufs)
        xt = get_x(len(tiles) - 1)

        def dsts(blk):
            g, kind = blk % 3, blk // 3
            return (sq, sk, sv)[kind][g]
        proj_A(xt, NSMP, dsts, kv_out_col=ST)
        for g in range(3):
            pr = nxt(tmpb, tbi)
            k.tt("dve", pr[:, 0:NSMP], sq[g], sk[g], ALU.mult)
            ps = nxt(pss, pi)
            k.mm(ps[:, 0:NSMP], ones_b, pr[:, 0:NSMP])
            ef = nxt(ostf, ofi)
            k.act(ef[:, 0:NSMP], ps[:, 0:NSMP], AF.Exp, scale=SCALE)
            if g == 0:
                k.copy("dve", denS, ef[:, 0:NSMP])
                k.tt("dve", numS, ef[:, 0:NSMP], sv[g], ALU.mult)
            else:
                k.tt("dve", denS, denS, ef[:, 0:NSMP], ALU.add)
                t2 = nxt(ostf, ofi)
                k.tt("dve", t2[:, 0:NSMP], ef[:, 0:NSMP], sv[g], ALU.mult)
                k.tt("dve", numS, numS, t2[:, 0:NSMP], ALU.add)
        for s in range(NSMP):
            cb = nxt(ckv, cki)
            k.dma("sp", cb.ap, cKV[s], writes=[cb])
            for g in range(3):
                psx = nxt(pss, pi)
                k.transpose(psx[:, 0:128], cb[:, g, 0, :], ident_f)
                k.copy("dve", cKT_b, psx[:, 0:128])
                k.copy("act", cV_b, cb[:, g, 1, :])
                att_unit(sq[g][:, s:s + 1], [(cKT_b, cV_b, None)], numS[:, s:s + 1], denS[:, s:s + 1], first=False)
        k.op("dve", lambda h: h.reciprocal(denS.ap, denS.ap), reads=[denS], writes=[denS])
        k.tt("dve", atto[:, 0:NSMP], numS, denS, ALU.mult)
        for (dst, so, nn, dbufs) in att_dst(T_, NSMP):
            k.dma("sp", dst, atto.ap[:, so:so + nn], reads=[atto], writes=dbufs)
        k.wait_all("sp", [atto] + ostf)
        return

    wBb = k.tile([128, KC, 1538], BF16, "wBb")
    k.dma("pool", [(wBb.ap[:, kc, :], wB[kc * 128:(kc + 1) * 128, :]) for kc in range(KC)], None, writes=[wBb])
    cpar_t = k.tile([128, 4, 5], F32, "cpar_t")
    k.dma("sp", cpar_t.ap, cpar, writes=[cpar_t])
    gpar_t = k.tile([128, 2], F32, "gpar_t")
    k.dma("sp", gpar_t.ap, gpar, writes=[gpar_t])
    gn_t = k.tile([128, 512], F32, "gn_t")
    k.dma("sp", gn_t.ap, gnorm, writes=[gn_t])
    ext = k.tile([128, 4, 3 + 512], F32, "ext")
    k.memset("dve", ext, 0.0)
    qkb = k.tile([128, 4, 512], BF16, "qkb")
    cacc = k.tile([128, 512], F32, "cacc")
    def mkstate(pfx):
        return {"Cst": k.tile([128, 2, 512], F32, "Cst" + pfx), "Cbf": k.tile([128, 2, 512], BF16, "Cbf" + pfx),
                "nst": k.tile([128, 2], F32, "nst" + pfx), "nbf": k.tile([128, 2], BF16, "nbf" + pfx),
                "mst": k.tile([128, 1], F32, "mst" + pfx)}
    cur = {"pfx": "P", "pss": pss[0:1], "pi": [0], "pst": pst[0:1], "pti": [0]}

    def nps():
        return nxt(cur["pss"], cur["pi"])

    def npt():
        return nxt(cur["pst"], cur["pti"])
    hmst = [k.tile([128, 4, 512], BF16, f"hmst{i}") for i in range(2)]
    hmi = [0]
    sm = {}

    def small(name, shape, dt=F32):
        key = (cur["pfx"], name, tuple(shape), dt)
        if key not in sm:
            sm[key] = [k.tile(shape, dt, f"s_{cur['pfx']}_{name}_{i}") for i in range(3 if name == "sotok" else 2)], [0]
        lst, ctr = sm[key]
        return nxt(lst, ctr)

    def bc_row(col, name):
        bcm = small(name + "_bc", [128, 128])
        k.ts("dve", bcm, ones_f, col, None, ALU.mult)
        ps = nps()
        k.mm(ps[:, 0:128], bcm, ident_f)
        return ps

    def stage_P(job):
        if job.get("pre_P"):
            yield from job["pre_P"]()
        xc, kT_c, pad, cx = job["xc"], job["kT_c"], job["pad"], job
        psv = nps()
        for kc in range(KC):
            k.mm(psv, xc[:, kc, :], wBb[:, kc, 512:1024], start=(kc == 0), stop=(kc == KC - 1))
        v_tok = small("vtok", [128, 512], BF16)
        k.copy("act", v_tok, psv)
        yield
        pso = nps()
        for kc in range(KC):
            k.mm(pso, xc[:, kc, :], wBb[:, kc, 1024:1536], start=(kc == 0), stop=(kc == KC - 1))
        so_tok = small("sotok", [128, 512], F32)
        k.act(so_tok, pso, AF.Sigmoid)
        k.tt("dve", so_tok, so_tok, gn_t, ALU.mult)
        yield
        psg = nps()
        for kc in range(KC):
            k.mm(psg[:, 0:2], xc[:, kc, :], wBb[:, kc, 1536:1538], start=(kc == 0), stop=(kc == KC - 1))
        ig = small("ig", [128, 1])
        k.tt("dve", ig, psg[:, 0:1], gpar_t[:, 0:1], ALU.add)
        z = small("z", [128, 1])
        k.tt("dve", z, psg[:, 1:2], gpar_t[:, 1:2], ALU.add)
        yield
        k.act(z, z, AF.Exp, scale=-1.0)
        k.act(z, z, AF.Ln, bias=1.0)
        lf = small("lf", [128, 1])
        k.ts("dve", lf, z, -1.0, None, ALU.mult)
        if pad:
            k.tt("dve", ig, ig, padneg, ALU.add)
            k.tt("dve", lf, lf, padone, ALU.mult)
        yield
        k_tok = small("ktok", [128, 256], BF16)
        pt = npt()
        for j in range(2):
            k.transpose(pt[:, j * 128:(j + 1) * 128], kT_c[j], ident_b)
        k.copy("dve", k_tok, pt[:, 0:256])
        yield
        psb = nps()
        k.mm(psb[:, 0:1], tri_f, lf)
        b_col = small("bcol", [128, 1])
        k.copy("dve", b_col, psb[:, 0:1])
        lfb = small("lfb", [128, 128])
        k.ts("dve", lfb, ones_f, lf, None, ALU.mult)
        yield
        psbr = nps()
        k.mm(psbr[:, 0:128], lfb, tri_f)
        bL = small("bL", [128, 1])
        k.copy("dve", bL, psbr[:, 127:128])
        a_col = small("acol", [128, 1])
        k.tt("dve", a_col, ig, b_col, ALU.subtract)
        yield
        psa = bc_row(a_col, "a")
        amax = small("amax", [128, 1])
        k.reduce("dve", amax, psa[:, 0:128], ALU.max)
        am = small("am", [128, 128])
        k.tt("dve", am, psa[:, 0:128], negm_f, ALU.add)
        cmx = small("cmx", [128, 1])
        k.reduce("dve", cmx, am, ALU.max)
        yield
        pscm = bc_row(cmx, "cm")
        cmrow = small("cmrow", [128, 128])
        k.copy("act", cmrow, pscm[:, 0:128])
        cmask = small("cmask", [128, 128])
        k.tt("dve", cmask, cmrow, posm_f, ALU.add)
        yield
        psqk = nps()
        for j in range(2):
            k.mm(psqk[:, 0:128], kT_c[j], job["qT_c"][j], start=(j == 0), stop=(j == 1))
        qk_sb = small("qksb", [128, 128])
        k.copy("act", qk_sb, psqk[:, 0:128])
        cx.update(v_tok=v_tok, so_tok=so_tok, k_tok=k_tok, b_col=b_col, bL=bL, a_col=a_col, amax=amax, cmx=cmx,
                  cmrow=cmrow, cmask=cmask, qk_sb=qk_sb)

    def stage_F(job):
        if job.get("pre_F"):
            yield from job["pre_F"]()
        stt_ = job["st"]
        Cst, Cbf, nst, nbf, mst = (stt_[n_] for n_ in ("Cst", "Cbf", "nst", "nbf", "mst"))
        qT_c, kT_c = job["qT_c"], job["kT_c"]
        v_tok, k_tok, b_col, bL, a_col, amax, cmx = (job[n_] for n_ in ("v_tok", "k_tok", "b_col", "bL", "a_col", "amax", "cmx"))
        cmrow, cmask, qk_sb = job["cmrow"], job["cmask"], job["qk_sb"]
        rn1 = small("rn1", [128, 128])
        k.ts("dve", rn1, cmask, mst, None, ALU.max)
        rn2 = small("rn2", [128, 128])
        k.ts("dve", rn2, cmrow, mst, None, ALU.max)
        mx = small("mx", [128, 1])
        k.tt("dve", mx, cmx, mst, ALU.max)
        m_col = small("mcol", [128, 1])
        k.tt("dve", m_col, b_col, mx, ALU.add)
        yield
        dT = small("dT", [128, 128])
        k.act(dT, rn1, AF.Exp, bias=a_col, scale=-1.0)
        wrow = small("wrow", [128, 128])
        k.act(wrow, rn2, AF.Exp, bias=mst, scale=-1.0)
        yield
        S_bf = small("Sbf", [128, 128], BF16)
        k.tt("dve", S_bf, qk_sb, dT, ALU.mult)
        qw = small("qw", [128, 2, 128], BF16)
        for j in range(2):
            k.tt("dve", qw[:, j, :], qT_c[j], wrow, ALU.mult)
        yield
        psn = nps()
        k.mm(psn, S_bf, v_tok, start=True, stop=False)
        for j in range(2):
            k.mm(psn, qw[:, j, :], Cbf[:, j, :], start=False, stop=(j == 1))
        psd = nps()
        k.mm(psd[:, 0:1], S_bf, ones_b[:, 0:1], start=True, stop=False)
        for j in range(2):
            k.mm(psd[:, 0:1], qw[:, j, :], nbf[:, j:j + 1], start=False, stop=(j == 1))
        t1 = small("t1", [128, 1])
        k.tt("dve", t1, amax, mst, ALU.max)
        mnew = small("mnew", [128, 1])
        k.tt("dve", mnew, bL, t1, ALU.add)
        bm = small("bm", [128, 1])
        k.tt("dve", bm, bL, mnew, ALU.subtract)
        yield
        wk = small("wk", [128, 1])
        k.act(wk, a_col, AF.Exp, bias=bm)
        dec = small("dec", [128, 1])
        k.tt("dve", dec, bm, mst, ALU.add)
        k.act(dec, dec, AF.Exp)
        kw = small("kw", [128, 256], BF16)
        k.ts("dve", kw, k_tok, wk, None, ALU.mult)
        dn = small("dn", [128, 1])
        k.act(dn, psd[:, 0:1], AF.Abs)
        em = small("em", [128, 1])
        k.act(em, m_col, AF.Exp, scale=-1.0)
        k.tt("dve", dn, dn, em, ALU.max)
        k.op("dve", lambda h, o=dn: h.reciprocal(o.ap, o.ap), reads=[dn], writes=[dn])
        hh = small("hh", [128, 512])
        k.ts("dve", hh, psn, dn, None, ALU.mult)
        job["hh"] = hh
        yield
        for j in range(2):
            psc = nps()
            k.mm(psc, kw[:, j * 128:(j + 1) * 128], v_tok)
            k.stt("dve", Cst[:, j, :], Cst[:, j, :], dec, psc, ALU.mult, ALU.add)
            k.copy("act", Cbf[:, j, :], Cst[:, j, :])
            psn2 = nps()
            k.mm(psn2[:, 0:1], kw[:, j * 128:(j + 1) * 128], ones_b[:, 0:1])
            k.stt("dve", nst[:, j:j + 1], nst[:, j:j + 1], dec, psn2[:, 0:1], ALU.mult, ALU.add)
            yield
        k.copy("dve", nbf, nst)
        k.copy("dve", mst, mnew)
        if job.get("post_F"):
            yield from job["post_F"]()

    def stage_N(job):
        hh, so_tok, hm_dst = job["hh"], job["so_tok"], job["hm_dst"]
        st6 = small("st6", [128, 6])
        k.op("dve", lambda h, o=st6, i=hh: h.bn_stats(o.ap, i.ap), reads=[hh], writes=[st6])
        mv = small("mv", [128, 2])
        k.op("dve", lambda h, o=mv, i=st6: h.bn_aggr(o.ap, i.ap), reads=[st6], writes=[mv])
        rs = small("rs", [128, 1])
        k.ts("dve", rs, mv[:, 1:2], LN_EPS, None, ALU.add)
        yield
        k.act(rs, rs, AF.Sqrt)
        k.op("dve", lambda h, o=rs: h.reciprocal(o.ap, o.ap), reads=[rs], writes=[rs])
        k.stt("dve", hh, hh, mv[:, 0:1], so_tok, ALU.subtract, ALU.mult)
        yield
        hb = small("hb", [128, 512], BF16)
        k.ts("dve", hb, hh, rs, None, ALU.mult)
        yield
        pt = npt()
        for c in range(4):
            k.transpose(pt[:, c * 128:(c + 1) * 128], hb[:, c * 128:(c + 1) * 128], ident_b)
        yield
        for c in range(4):
            k.copy("act", hm_dst[:, c, :], pt[:, c * 128:(c + 1) * 128])
        if job.get("post_N"):
            yield from job["post_N"]()

    def conv_silu(n, ext_taps, dst_qk):
        for blk in range(4):
            k.act(cacc[:, 0:n], ext_taps(blk, 0), AF.Identity, bias=cpar_t[:, blk, 4:5], scale=cpar_t[:, blk, 0:1])
            for j in range(1, 4):
                k.stt("dve", cacc[:, 0:n], ext_taps(blk, j), cpar_t[:, blk, j:j + 1], cacc[:, 0:n], ALU.mult, ALU.add)
            if blk < 2:
                k.act(dst_qk(blk), cacc[:, 0:n], AF.Silu)
            else:
                k.act(cacc[:, 0:n], cacc[:, 0:n], AF.Silu)
                k.ts("dve", dst_qk(blk), cacc[:, 0:n], 256.0 ** -0.5, None, ALU.mult)

    def proj_qk(xt, n, dst):
        for blk in range(4):
            ps = nps()
            for kc in range(KC):
                k.mm(ps[:, 0:n], wBb[:, kc, blk * 128:(blk + 1) * 128], xt[:, kc, 0:n], start=(kc == 0), stop=(kc == KC - 1))
            k.copy("act", dst(blk), ps[:, 0:n])

    qkbs = [qkb, k.tile([128, 4, 512], BF16, "qkb1")]
    SP_ = mkstate("P")
    SS2 = [mkstate("S0"), mkstate("S1")]
    xs_t = k.tile([128, KC, NSMP], BF16, "xs_t")
    exts = k.tile([128, 4, 4, NSMP], F32, "exts")
    sqk = k.tile([128, 4, NSMP], BF16, "sqk")
    n0_t = k.tile([128, 2, NSMP], F32, "n0_t")
    m0_t = k.tile([128, NSMP], F32, "m0_t")
    sn_t = k.tile([128, 2, NSMP], F32, "sn_t")
    sm_t = k.tile([128, NSMP], F32, "sm_t")
    xpad = k.tile([128, KC, 128], BF16, "xpad")
    qkpads = [k.tile([128, 4, 128], BF16, f"qkpad{i}") for i in range(2)]
    hmss = [k.tile([128, 4, 128], BF16, f"hms{i}") for i in range(2)]
    hmso = k.tile([128, 4, NSMP], BF16, "hmso")
    for nm in ("Cst", "Cbf", "nst", "nbf", "mst"):
        k.memset("dve", SP_[nm], 0.0)
    jobs = []
    for ti, (t0, n) in enumerate(tiles[:-1]):
        qb = qkbs[ti % 2]
        hst = hmst[ti % 2]
        hold = {}

        def pre_tile(t0=t0, qb=qb, hold=hold):
            xt = get_x(t0 // 512)
            hold["xt"] = xt
            proj_qk(xt, 512, lambda blk: ext[:, blk, 3:3 + 512])
            yield
            conv_silu(512, lambda blk, j: ext[:, blk, j:j + 512], lambda blk: qb[:, blk, :])
            if t0 + 512 == T_:
                k.dma("sp", conv_o, ext.ap[:, :, 512:515], reads=[ext])
            halo = small("halo", [128, 4, 3])
            k.copy("dve", halo, ext[:, :, 512:515])
            k.copy("dve", ext[:, :, 0:3], halo)
            yield

        def post_tile(t0=t0, hst=hst):
            for c in range(4):
                for (dst, so, nn, dbufs) in hm_dst(c, t0, 512):
                    k.dma("sp", dst, hst.ap[:, c, so:so + nn], reads=[hst], writes=dbufs)
            if io.get("after_tile"):
                io["after_tile"](t0)
            yield
        for c in range(4):
            sl = slice(c * 128, (c + 1) * 128)
            job = {"kind": "p", "pad": False, "st": SP_, "sl": sl, "hold": hold,
                   "qT_c": [qb[:, 0, sl], qb[:, 1, sl]], "kT_c": [qb[:, 2, sl], qb[:, 3, sl]], "hm_dst": hst[:, :, sl]}
            if c == 0:
                job["pre_P"] = pre_tile
            if c == 3:
                job["post_N"] = post_tile
            jobs.append(job)

    def final_prompt():
        k.dma("sp", pc_o.rearrange("(j p) v -> p j v", p=128), SP_["Cst"].ap, reads=[SP_["Cst"]])
        k.dma("sp", pn_o, SP_["nst"].ap, reads=[SP_["nst"]])
        k.dma("sp", pm_o, SP_["mst"].ap, reads=[SP_["mst"]])
        yield
    jobs[-1]["post_F"] = final_prompt

    def sample_setup():
        k.dma("pool", [(xs_t.ap[:, kc, :], xT[kc * 128:(kc + 1) * 128, T_:T_ + NSMP]) for kc in range(KC)], None, writes=[xs_t])
        k.dma("sp", exts.ap[:, :, 0:3, :], cst, writes=[exts])
        proj_qk(xs_t, NSMP, lambda blk: exts[:, blk, 3, :])
        yield
        conv_silu(NSMP, lambda blk, j: exts[:, blk, j, :], lambda blk: sqk[:, blk, :])
        k.dma("sp", sconv_o, exts.ap[:, :, 1:4, :], reads=[exts])
        k.dma("sp", n0_t.ap, n0, writes=[n0_t])
        k.dma("sp", m0_t.ap, m0, writes=[m0_t])
        k.memset("dve", xpad, 0.0)
        for qp in qkpads:
            k.memset("dve", qp, 0.0)
        yield
    for s_ in range(NSMP):
        stS = SS2[s_ % 2]
        qp = qkpads[s_ % 2]
        hms = hmss[s_ % 2]

        def pre_s(s_=s_, stS=stS, qp=qp):
            if s_ == 0:
                yield from sample_setup()
            k.dma("sp", stS["Cst"].ap, c0[s_].rearrange("(j p) v -> p j v", p=128), writes=[stS["Cst"]])
            k.copy("act", stS["Cbf"], stS["Cst"])
            k.copy("dve", stS["nst"], n0_t[:, :, s_])
            k.copy("dve", stS["nbf"], stS["nst"])
            k.copy("dve", stS["mst"], m0_t[:, s_:s_ + 1])
            k.copy("dve", xpad[:, :, 0:1], xs_t[:, :, s_:s_ + 1])
            k.copy("dve", qp[:, :, 0:1], sqk[:, :, s_:s_ + 1])
            yield

        def post_Fs(s_=s_, stS=stS):
            k.dma("sp", sc_o[s_].rearrange("(j p) v -> p j v", p=128), stS["Cst"].ap, reads=[stS["Cst"]])
            k.copy("dve", sn_t[:, :, s_], stS["nst"])
            k.copy("dve", sm_t[:, s_:s_ + 1], stS["mst"])
            yield

        def post_Ns(s_=s_, hms=hms):
            k.copy("dve", hmso[:, :, s_:s_ + 1], hms[:, :, 0:1])
            yield
        jobs.append({"kind": "s", "pad": True, "st": stS, "xc": xpad, "qT_c": [qp[:, 0, :], qp[:, 1, :]],
                     "kT_c": [qp[:, 2, :], qp[:, 3, :]], "hm_dst": hms, "pre_P": pre_s, "post_F": post_Fs, "post_N": post_Ns})

    def gen_P(job):
        if job["kind"] == "p":
            def lazy():
                if job.get("pre_P"):
                    yield from job["pre_P"]()
                job["xc"] = job["hold"]["xt"][:, :, job["sl"]]
                job2 = dict(job)
                job2.pop("pre_P", None)
                yield from stage_P(job2)
                for kk_ in ("v_tok", "so_tok", "k_tok", "b_col", "bL", "a_col", "amax", "cmx", "cmrow", "cmask", "qk_sb"):
                    job[kk_] = job2[kk_]
            return lazy()
        return stage_P(job)

    CTX = {"P": {"pfx": "P", "pss": pss[0:1], "pi": [0], "pst": pst[0:1], "pti": [0]},
           "F": {"pfx": "F", "pss": pss[1:3], "pi": [0], "pst": pst[0:1], "pti": [0]},
           "N": {"pfx": "N", "pss": pss[1:3], "pi": [0], "pst": pst[0:1], "pti": [0]},
           "sP": {"pfx": "sP", "pss": pss[3:4], "pi": [0], "pst": pst[1:2], "pti": [0]},
           "sF": {"pfx": "sF", "pss": pss[4:6], "pi": [0], "pst": pst[1:2], "pti": [0]},
           "sN": {"pfx": "sN", "pss": pss[4:6], "pi": [0], "pst": pst[1:2], "pti": [0]}}

    def advance(g, cname):
        cur.clear()
        cur.update(CTX[cname])
        try:
            next(g)
            return True
        except StopIteration:
            return False

    def stage_gens(jl, step, names):
        gens = []
        if 0 <= step - 2 < len(jl):
            gens.append([stage_N(jl[step - 2]), names[2]])
        if 0 <= step - 1 < len(jl):
            gens.append([stage_F(jl[step - 1]), names[1]])
        if step < len(jl):
            gens.append([gen_P(jl[step]), names[0]])
        return gens
    pjobs = [j for j in jobs if j["kind"] == "p"]
    sjobs = [j for j in jobs if j["kind"] == "s"]
    ratio = max(1, len(pjobs) // max(1, len(sjobs)))
    carry = []
    nsteps = max(len(pjobs) + 2, ratio * (len(sjobs) + 2))
    for step in range(nsteps):
        if step % ratio == 0:
            for gc in carry:
                while advance(gc[0], gc[1]):
                    pass
            carry = stage_gens(sjobs, step // ratio, ("sP", "sF", "sN"))
        gens = stage_gens(pjobs, step, ("P", "F", "N"))
        tick = 0
        while gens:
            gens = [gc for gc in gens if advance(gc[0], gc[1])]
            tick += 1
            if tick % ratio == 0:
                carry = [gc for gc in carry if advance(gc[0], gc[1])]
    for gc in carry:
        while advance(gc[0], gc[1]):
            pass
    for c in range(4):
        for (dst, so, nn, dbufs) in hm_dst(c, T_, NSMP):
            k.dma("sp", dst, hmso.ap[:, c, so:so + nn], reads=[hmso], writes=dbufs)
    k.dma("sp", sn_o, sn_t.ap, reads=[sn_t])
    k.dma("sp", sm_o, sm_t.ap, reads=[sm_t])
    k.wait_all("sp", [hmso, sn_t, sm_t, exts, ext] + hmst + [SP_[n_] for n_ in ("Cst", "nst", "mst")]
               + [st_[n_] for st_ in SS2 for n_ in ("Cst", "nst", "mst")])


def build_fused(T_=8192, NSMP=32, DFF=8192, ST1=2048, ST2=1024):
    nc = bass.Bass("TRN2", target_bir_lowering=False)
    NPq = T_ // 4
    HW = T_ // 2
    io = {}
    l1_io(nc, "A", T_, NSMP, ST1, io)
    l1_io(nc, "B", T_, NSMP, ST1, io)
    l2_io(nc, NPq, 4, DFF, io, fused=True)
    k = KB(nc)
    X, Gt = {}, {}
    for kk in range(5):
        for half in range(2):
            X[(kk, half)] = (nc.dram_tensor(f"X{kk}_{half}", [128, HW], BF16), Buf(f"X{kk}_{half}"))
            Gt[(kk, half)] = (nc.dram_tensor(f"G{kk}_{half}", [512, HW], BF16), Buf(f"G{kk}_{half}"))
    XS = (nc.dram_tensor("XS", [128, 5 * NSMP], BF16), Buf("XS"))
    GS = (nc.dram_tensor("GS", [512, 5 * NSMP], BF16), Buf("GS"))

    def xdst(kk, c0, n):
        if c0 >= T_:
            return [(XS[0].ap()[:, kk * NSMP:(kk + 1) * NSMP], 0, n, [XS[1]])]
        out = []
        so = 0
        while n > 0:
            half = c0 // HW
            nn = min(n, (half + 1) * HW - c0)
            xt, xb = X[(kk, half)]
            out.append((xt.ap()[:, c0 - half * HW:c0 - half * HW + nn], so, nn, [xb]))
            c0 += nn
            so += nn
            n -= nn
        return out
    io["att_dst"] = lambda c0, n: xdst(0, c0, n)
    io["hm_dst"] = lambda c, c0, n: xdst(1 + c, c0, n)
    sh = make_shared(k, io["cm"])
    groups = [[0, 1, 2, 3], [4, 5, 6, 7]]
    k.push_scope()
    emit_l1(k, "A", io, sh, T_, NSMP, ST1)
    k.pop_scope()
    for half in range(2):
        k.collective("AllGather", X[(0, half)][0], Gt[(0, half)][0], groups, reads=[X[(0, half)][1]], writes=[Gt[(0, half)][1]])
    def after_tile(tend):
        for half in range(1):
            if tend == (half + 1) * HW:
                for kk in range(1, 5):
                    k.collective("AllGather", X[(kk, half)][0], Gt[(kk, half)][0], groups,
                                 reads=[X[(kk, half)][1]], writes=[Gt[(kk, half)][1]])
    io["after_tile"] = after_tile
    k.push_scope()
    emit_l1(k, "B", io, sh, T_, NSMP, ST1)
    k.pop_scope()
    for kk in range(1, 5):
        k.collective("AllGather", X[(kk, 1)][0], Gt[(kk, 1)][0], groups, reads=[X[(kk, 1)][1]], writes=[Gt[(kk, 1)][1]])
    k.collective("AllGather", XS[0], GS[0], groups, reads=[XS[1]], writes=[GS[1]])
    G = {"g": {key: (t.ap(), b) for key, (t, b) in Gt.items()}, "gs": (GS[0].ap(), GS[1])}
    k.push_scope()
    emit_l2(k, io, sh, NPq, 4, DFF, ST2, G=G)
    k.pop_scope()
    k.emit()
    return nc


GROUPS = ((128, 1), (512, 4), (2048, 16))
NEG = -1.0e30

def const_masks():
    cm = np.zeros((128, 6, 128), np.float32)
    i = np.arange(128)
    cm[:, 0, :] = np.eye(128)
    cm[:, 1, :] = (i[:, None] <= i[None, :])
    cm[:, 2, :] = (i[:, None] >= i[None, :])
    cm[:, 3, :] = np.where(i[None, :] <= i[:, None], 0.0, NEG)
    cm[:, 5, :] = np.where(i[:, None] <= i[None, :], 0.0, -NEG)
    cm[:, 4, 0] = np.where(i == 0, 0.0, NEG)
    cm[:, 4, 1] = (i == 0)
    return cm

def colsA(h):
    c = []
    for base in (0, 1536, 3072):
        for g in range(3):
            c.append(np.arange(base + (4 * g + h) * 128, base + (4 * g + h + 1) * 128))
    return np.concatenate(c)

def colsB(h):
    o = 4608
    mq = np.arange(o + h * 256, o + (h + 1) * 256)
    mk = np.arange(o + 1024 + h * 256, o + 1024 + (h + 1) * 256)
    mv = np.arange(o + 2048 + h * 512, o + 2048 + (h + 1) * 512)
    mo = np.arange(o + 4096 + h * 512, o + 4096 + (h + 1) * 512)
    mi = np.array([o + 6144 + h]); mf = np.array([o + 6148 + h])
    return np.concatenate([mq, mk, mv, mo, mi, mf])

def qkcols(h):
    return np.concatenate([np.arange(h * 256, (h + 1) * 256), np.arange(1024 + h * 256, 1024 + (h + 1) * 256)])

def prep_l1(h, xb, xs, inp):
    w_in = inp["w_in"][0]
    xT = np.ascontiguousarray(np.concatenate([xb, xs], 0).T)
    cm = const_masks()
    cK = []
    for g, (win, d) in enumerate(GROUPS):
        c = inp[f"cache_kv_g{g}"][0]
        cK.append(c[:, 0::d, :, h, :])
    cKV = np.ascontiguousarray(np.stack(cK, axis=2))
    inA = {"xT": xT, "cm": cm, "wA": np.ascontiguousarray(w_in[:, colsA(h)]), "cKV": cKV}
    qk = qkcols(h)
    cst = inp["state_mlstm_conv"][0][:, :, qk]
    cst = np.ascontiguousarray(cst.reshape(32, 3, 4, 128).transpose(3, 2, 1, 0))
    wc = inp["w_conv"][0][:, qk].reshape(4, 4, 128)
    bc = inp["b_conv"][0][qk].reshape(4, 128)
    cpar = np.ascontiguousarray(np.concatenate([wc.transpose(2, 1, 0), bc.T[:, :, None]], axis=2)).astype(np.float32)
    gpar = np.ascontiguousarray(np.broadcast_to(np.array([inp["b_igate"][0, h], inp["b_fgate"][0, h]], np.float32), (128, 2)))
    gnorm = np.ascontiguousarray(np.broadcast_to(inp["g_mlstm_norm"][0, h * 512:(h + 1) * 512], (128, 512)))
    n0 = np.ascontiguousarray(inp["state_mlstm_n"][0][:, h].reshape(32, 2, 128).transpose(2, 1, 0))
    m0 = np.ascontiguousarray(np.broadcast_to(inp["state_mlstm_m"][0][:, h], (128, 32)))
    inB = {"xT": xT, "cm": cm, "wB": np.ascontiguousarray(w_in[:, colsB(h)]), "cst": cst,
           "c0": np.ascontiguousarray(inp["state_mlstm_c"][0][:, h]), "n0": n0, "m0": m0, "cpar": cpar, "gpar": gpar, "gnorm": gnorm}
    return inA, inB


import ml_dtypes
from concourse.bass_utils import run_bass_kernel_spmd

_PROGS = {}


def _onehot_ident(n, j):
    m = np.zeros((128, n, 128), np.float32)
    m[:, j, :] = np.eye(128, dtype=np.float32)
    return m.astype(ml_dtypes.bfloat16)


def kernel(**inp):
    inp = {k_: np.asarray(v) for k_, v in inp.items()}
    B, T_, NS = 2, inp["x_prompt"].shape[1], 32
    NPq = T_ // 4
    xs = inp["x_sample"][:, 0]
    cores = [(b, h) for b in range(B) for h in range(4)]
    lns = [inp[n][0] for n in ("ln1_g", "ln1_b", "ln2_g", "ln2_b", "ln3_g", "ln3_b")]
    lnp = np.ascontiguousarray(np.stack([l.reshape(16, 128).T for l in lns], axis=1)).astype(np.float32)
    wg = np.ascontiguousarray(inp["w_in"][0][:, 10760:14856])
    in_maps = []
    for c, (b, q) in enumerate(cores):
        inA, inB = prep_l1(q, inp["x_prompt"][b], xs, inp)
        m = dict(inA)
        m.update(inB)
        tsl = slice(NPq * q, NPq * (q + 1))
        m.update({
            "x2T": np.ascontiguousarray(np.concatenate([inp["x_prompt"][b, tsl], xs[4 * c:4 * c + 4]], 0).T),
            "selq": _onehot_ident(4, q), "sels": _onehot_ident(8, c), "wg": wg,
            "w_br_a": inp["w_branch_att"][0], "w_br_m": inp["w_branch_mlstm"][0], "w_mix": inp["w_mix_out"][0],
            "w_cq": inp["w_cross_q"][0], "w_mk": inp["w_mem_k"][0], "w_mv": inp["w_mem_v"][0],
            "w_co": inp["w_cross_out"][0], "w_up": inp["w_up"][0], "w_down": inp["w_down"][0],
            "memT": np.ascontiguousarray(inp["mem_prompt"][b].T),
            "skv": np.ascontiguousarray(inp["cache_mem_kv"][0][4 * c:4 * c + 4].reshape(4, 256, 2, 512)),
            "lnp": lnp, "ident": np.eye(128, dtype=np.float32)})
        in_maps.append(m)
    if T_ not in _PROGS:
        _PROGS[T_] = build_fused(T_=T_)
    res = run_bass_kernel_spmd(_PROGS[T_], in_maps, core_ids=list(range(8))).results
    res = [{k_: np.asarray(v) for k_, v in r.items()} for r in res]
    f = np.float32
    ST1 = 2048
    y_p = np.zeros((B, T_, 2048), f)
    y_s = np.zeros((NS, 1, 2048), f)
    keeps = tuple(min(w, T_) for w in (128, 512, 2048))
    p_kv = [np.zeros((1, B, kp, 2, 4, 128), f) for kp in keeps]
    p_mem = np.zeros((1, B, 256, 2, 4, 128), f)
    p_conv = np.zeros((1, B, 3, 2048), f)
    p_c = np.zeros((1, B, 4, 256, 512), f)
    p_n = np.zeros((1, B, 4, 256), f)
    p_m = np.zeros((1, B, 4), f)
    s_kv = [np.zeros((1, NS, 1, 2, 4, 128), f) for _ in range(3)]
    s_conv = np.zeros((1, NS, 3, 2048), f)
    s_c = np.zeros((1, NS, 4, 256, 512), f)
    s_n = np.zeros((1, NS, 4, 256), f)
    s_m = np.zeros((1, NS, 4), f)
    for c, (b, h) in enumerate(cores):
        r = res[c]
        yT = r["yT"]
        y_p[b, NPq * h:NPq * (h + 1)] = yT[:, :NPq].T
        y_s[4 * c:4 * c + 4, 0] = yT[:, NPq:NPq + 4].T
        if h == 0:
            p_mem[0, b] = r["pmem"].reshape(256, 2, 4, 128)
        kvT = r["kvT_o"]
        for g, kp in enumerate(keeps):
            for kv in range(2):
                p_kv[g][0, b, :, kv, h, :] = kvT[g, kv, :, ST1 - kp:ST1].T
                if b == 0:
                    s_kv[g][0, :, 0, kv, h, :] = kvT[g, kv, :, ST1:ST1 + NS].T
        qk = qkcols(h)
        p_conv[0, b][:, qk] = r["conv_o"].transpose(2, 1, 0).reshape(3, 512)
        p_c[0, b, h] = r["pc_o"]
        p_n[0, b, h] = r["pn_o"].T.reshape(256)
        p_m[0, b, h] = r["pm_o"][0, 0]
        if b == 0:
            s_conv[0][:, :, qk] = r["sconv_o"].transpose(3, 2, 1, 0).reshape(NS, 3, 512)
            s_c[0, :, h] = r["sc_o"]
            s_n[0, :, h] = r["sn_o"].transpose(2, 1, 0).reshape(NS, 256)
            s_m[0, :, h] = r["sm_o"][0]
    return (y_p, y_s, p_kv[0], p_kv[1], p_kv[2], p_mem, p_conv, p_c, p_n, p_m,
            s_kv[0], s_kv[1], s_kv[2], s_conv, s_c, s_n, s_m)
```
